# Optimizing a Trainium2 kernel written in Bass

```python
import math
import jax, jax.numpy as jnp
from jax import lax
import numpy as np

D_MODEL = 1024
BATCH = 4
SEQ = 4096
DEPTH = 1
DEC_BATCH = 32
DEC_SEQ = 8
PAST_LEN = 8192
PAGE_SIZE = 128

HEAD_DIM = 64
HEADS_PER_GROUP = 4
ATTN_GROUPS = ((128, 1), (512, 4), (2048, 16))
N_HEADS = HEADS_PER_GROUP * len(ATTN_GROUPS)
ATTN_WIDTH = N_HEADS * HEAD_DIM
ATTN_OUT_WIDTH = HEADS_PER_GROUP * HEAD_DIM
N_BUCKETS = 32
MAX_DISTANCE = 2048
QUERY_BLOCK = 128
POOL_WINDOWS = (2, 4, 8, 16)
POOL_WIDTH = D_MODEL // 2
POOL_GROUP_WIDTH = POOL_WIDTH // len(POOL_WINDOWS)
POOL_STATE = max(POOL_WINDOWS) - 1
D_FF = ((8 * D_MODEL // 3 + 127) // 128) * 128
EPS = 1e-6
IN_WIDTH = 3 * ATTN_WIDTH + POOL_WIDTH + 2 * D_MODEL
IN_SPLITS = (ATTN_WIDTH, 2 * ATTN_WIDTH, 3 * ATTN_WIDTH,
             3 * ATTN_WIDTH + POOL_WIDTH, 3 * ATTN_WIDTH + POOL_WIDTH + D_MODEL)

kernel_name = "hybrid_dilated_attn_pool_macaron_step"


def _rmsnorm(x, g):
    xf = x.astype(jnp.float32)
    y = xf * lax.rsqrt(jnp.mean(xf * xf, axis=-1, keepdims=True) + EPS)
    return (y * g.astype(jnp.float32)).astype(x.dtype)


def _head_rmsnorm(x, g):
    xf = x.astype(jnp.float32)
    y = xf * lax.rsqrt(jnp.mean(xf * xf, axis=-1, keepdims=True) + EPS)
    return (y * g.astype(jnp.float32)[None, None]).astype(x.dtype)


def _swiglu(x, w_up, w_down):
    gate, up = jnp.split(x @ w_up, 2, axis=-1)
    return (jax.nn.silu(gate) * up) @ w_down


def _t5_buckets(distance):
    max_exact = N_BUCKETS // 2
    d = np.asarray(distance, dtype=np.int32)
    ratio = np.log(np.maximum(d, 1).astype(np.float32) / np.float32(max_exact))
    large = max_exact + (ratio / np.float32(math.log(MAX_DISTANCE / max_exact))
                         * (N_BUCKETS - max_exact)).astype(np.int32)
    large = np.minimum(large, N_BUCKETS - 1)
    return np.where(d < max_exact, d, large).astype(np.int32)


def _dilated_attention(q, k, v, past_kv, rel_bias_table):
    B, T = q.shape[:2]
    qb = math.gcd(T, QUERY_BLOCK)
    n_blocks = T // qb
    scale = HEAD_DIM ** -0.5
    neg = jnp.finfo(jnp.float32).min
    kv_new = jnp.stack([k, v], axis=2)
    kv_alls, offsets, biases, new_state = [], [], [], []
    for g, (win, dil) in enumerate(ATTN_GROUPS):
        kv_g = kv_new[:, :, :, g * HEADS_PER_GROUP:(g + 1) * HEADS_PER_GROUP]
        if past_kv is None:
            kv_all = kv_g
            keep = min(win, T)
        else:
            kv_all = jnp.concatenate([past_kv[g].astype(kv_g.dtype), kv_g], axis=1)
            keep = past_kv[g].shape[1]
        new_state.append(kv_all[:, kv_all.shape[1] - keep:])
        kv_alls.append(kv_all)
        offsets.append(kv_all.shape[1] - T)
        dist = dil * np.arange(win // dil + 1)
        biases.append(rel_bias_table[_t5_buckets(dist)][:, g * HEADS_PER_GROUP:(g + 1) * HEADS_PER_GROUP]
                      .astype(jnp.float32))

    def block(i):
        start = i * qb
        q_blk = lax.dynamic_slice_in_dim(q, start, qb, axis=1)
        outs, lses = [], []
        for g, (win, dil) in enumerate(ATTN_GROUPS):
            n_keys = win // dil + 1
            idx = offsets[g] + start + jnp.arange(qb)[:, None] - dil * jnp.arange(n_keys)[None, :]
            valid = idx >= 0
            kv_blk = jnp.take(kv_alls[g], jnp.maximum(idx, 0), axis=1)
            qg = q_blk[:, :, g * HEADS_PER_GROUP:(g + 1) * HEADS_PER_GROUP].astype(jnp.float32)
            logits = jnp.einsum('bqhd,bqjhd->bqjh', qg, kv_blk[:, :, :, 0].astype(jnp.float32)) * scale
            logits = jnp.where(valid[None, :, :, None], logits + biases[g][None, None], neg)
            m = jnp.max(logits, axis=2, keepdims=True)
            p = jnp.exp(logits - m)
            denom = jnp.sum(p, axis=2)
            o = jnp.einsum('bqjh,bqjhd->bqhd', p, kv_blk[:, :, :, 1].astype(jnp.float32)) / denom[..., None]
            outs.append(o)
            lses.append(m[:, :, 0] + jnp.log(denom))
        wts = jax.nn.softmax(jnp.stack(lses, axis=0), axis=0)
        o = jnp.sum(wts[..., None] * jnp.stack(outs, axis=0), axis=0)
        return o.astype(q.dtype)

    o = lax.map(block, jnp.arange(n_blocks))
    o = o.transpose(1, 0, 2, 3, 4).reshape(B, T, ATTN_OUT_WIDTH)
    return o, new_state


def _multiscale_pool(u, past, pos0, w_group, scale):
    B, T, C = u.shape
    ue = jnp.concatenate([past.astype(u.dtype), u], axis=1)
    cs = jnp.cumsum(ue.astype(jnp.float32), axis=1)
    cs = jnp.concatenate([jnp.zeros((B, 1, C), jnp.float32), cs], axis=1)
    pos = pos0 + jnp.arange(T)
    outs = []
    for gi, win in enumerate(POOL_WINDOWS):
        sl = slice(gi * POOL_GROUP_WIDTH, (gi + 1) * POOL_GROUP_WIDTH)
        s = cs[:, POOL_STATE + 1:, sl] - cs[:, POOL_STATE + 1 - win:POOL_STATE + 1 - win + T, sl]
        cnt = jnp.minimum(pos + 1, win).astype(jnp.float32)[None, :, None]
        outs.append(s / cnt - u[:, :, sl].astype(jnp.float32))
    d = jnp.stack(outs, axis=2)
    y = jnp.einsum('btgc,gcd->btgd', d, w_group.astype(jnp.float32)).reshape(B, T, C)
    y = y * scale.astype(jnp.float32)
    return y.astype(u.dtype), ue[:, ue.shape[1] - POOL_STATE:]


def _layer(x, pos0, past_kv, past_pool, w, rel_bias_table):
    B, T, _ = x.shape
    x = x + 0.5 * _swiglu(_rmsnorm(x, w["norm_ffn1"]), w["ffn1_w_up"], w["ffn1_w_down"])
    h = _rmsnorm(x, w["norm_mix"])
    q, k, v, u, g_a, g_b = jnp.split(h @ w["w_in"], IN_SPLITS, axis=-1)
    q = _head_rmsnorm(q.reshape(B, T, N_HEADS, HEAD_DIM), w["q_norm"])
    k = _head_rmsnorm(k.reshape(B, T, N_HEADS, HEAD_DIM), w["k_norm"])
    v = v.reshape(B, T, N_HEADS, HEAD_DIM)
    attn, new_kv = _dilated_attention(q, k, v, past_kv, rel_bias_table)
    pooled, new_pool = _multiscale_pool(u, past_pool, pos0, w["pool_w_group"], w["pool_scale"])
    merged = (jax.nn.sigmoid(g_a) * (attn @ w["w_attn_branch"])
              + jax.nn.sigmoid(g_b) * (pooled @ w["w_pool_branch"]))
    x = x + merged @ w["w_out"]
    x = x + 0.5 * _swiglu(_rmsnorm(x, w["norm_ffn2"]), w["ffn2_w_up"], w["ffn2_w_down"])
    return x, new_kv, new_pool


def setup_inputs(seed: int = 0) -> dict:
    key = jax.random.key(seed)
    ks = jax.random.split(key, 24)
    f32 = jnp.float32

    def nrm(k, shape, s=1.0):
        return jax.random.normal(k, shape, f32) * s

    def gain(k, shape):
        return 1.0 + 0.05 * jax.random.normal(k, shape, f32)

    win_len = [min(win, PAST_LEN) for win, _ in ATTN_GROUPS]
    return {
        "x_prompt": nrm(ks[0], (BATCH, SEQ, D_MODEL)),
        "x_sample": nrm(ks[1], (DEC_BATCH, DEC_SEQ, D_MODEL)),
        "cache_kv_w128": nrm(ks[2], (DEPTH, DEC_BATCH, win_len[0], 2, HEADS_PER_GROUP, HEAD_DIM)),
        "cache_kv_w512": nrm(ks[3], (DEPTH, DEC_BATCH, win_len[1], 2, HEADS_PER_GROUP, HEAD_DIM)),
        "cache_kv_w2048": nrm(ks[4], (DEPTH, DEC_BATCH, win_len[2], 2, HEADS_PER_GROUP, HEAD_DIM)),
        "state_pool": nrm(ks[5], (DEPTH, DEC_BATCH, POOL_STATE, POOL_WIDTH)),
        "rel_bias_table": nrm(ks[6], (N_BUCKETS, N_HEADS), 0.5),
        "norm_ffn1": gain(ks[7], (DEPTH, D_MODEL)),
        "ffn1_w_up": nrm(ks[8], (DEPTH, D_MODEL, 2 * D_FF), D_MODEL ** -0.5),
        "ffn1_w_down": nrm(ks[9], (DEPTH, D_FF, D_MODEL), D_FF ** -0.5),
        "norm_mix": gain(ks[10], (DEPTH, D_MODEL)),
        "w_in": nrm(ks[11], (DEPTH, D_MODEL, IN_WIDTH), D_MODEL ** -0.5),
        "q_norm": gain(ks[12], (DEPTH, N_HEADS, HEAD_DIM)),
        "k_norm": gain(ks[13], (DEPTH, N_HEADS, HEAD_DIM)),
        "pool_w_group": nrm(ks[14], (DEPTH, len(POOL_WINDOWS), POOL_GROUP_WIDTH, POOL_GROUP_WIDTH),
                            POOL_GROUP_WIDTH ** -0.5),
        "pool_scale": gain(ks[15], (DEPTH, POOL_WIDTH)),
        "w_attn_branch": nrm(ks[16], (DEPTH, ATTN_OUT_WIDTH, D_MODEL), ATTN_OUT_WIDTH ** -0.5),
        "w_pool_branch": nrm(ks[17], (DEPTH, POOL_WIDTH, D_MODEL), POOL_WIDTH ** -0.5),
        "w_out": nrm(ks[18], (DEPTH, D_MODEL, D_MODEL), D_MODEL ** -0.5),
        "norm_ffn2": gain(ks[19], (DEPTH, D_MODEL)),
        "ffn2_w_up": nrm(ks[20], (DEPTH, D_MODEL, 2 * D_FF), D_MODEL ** -0.5),
        "ffn2_w_down": nrm(ks[21], (DEPTH, D_FF, D_MODEL), D_FF ** -0.5),
    }


def reference(x_prompt, x_sample, cache_kv_w128, cache_kv_w512, cache_kv_w2048, state_pool,
              rel_bias_table, norm_ffn1, ffn1_w_up, ffn1_w_down, norm_mix, w_in, q_norm, k_norm,
              pool_w_group, pool_scale, w_attn_branch, w_pool_branch, w_out,
              norm_ffn2, ffn2_w_up, ffn2_w_down):
    xp, xs = x_prompt, x_sample
    kv_p, kv_s = ([], [], []), ([], [], [])
    pool_p, pool_s = [], []
    for layer in range(DEPTH):
        w = dict(norm_ffn1=norm_ffn1[layer], ffn1_w_up=ffn1_w_up[layer], ffn1_w_down=ffn1_w_down[layer],
                 norm_mix=norm_mix[layer], w_in=w_in[layer], q_norm=q_norm[layer], k_norm=k_norm[layer],
                 pool_w_group=pool_w_group[layer], pool_scale=pool_scale[layer],
                 w_attn_branch=w_attn_branch[layer], w_pool_branch=w_pool_branch[layer],
                 w_out=w_out[layer], norm_ffn2=norm_ffn2[layer], ffn2_w_up=ffn2_w_up[layer],
                 ffn2_w_down=ffn2_w_down[layer])
        zero_pool = jnp.zeros((xp.shape[0], POOL_STATE, POOL_WIDTH), xp.dtype)
        xp, nkv_p, npool_p = _layer(xp, 0, None, zero_pool, w, rel_bias_table)
        past = (cache_kv_w128[layer], cache_kv_w512[layer], cache_kv_w2048[layer])
        xs, nkv_s, npool_s = _layer(xs, PAST_LEN, past, state_pool[layer], w, rel_bias_table)
        for g in range(len(ATTN_GROUPS)):
            kv_p[g].append(nkv_p[g])
            kv_s[g].append(nkv_s[g])
        pool_p.append(npool_p)
        pool_s.append(npool_s)
    kv128_prompt = jnp.stack(kv_p[0], axis=0)
    kv512_prompt = jnp.stack(kv_p[1], axis=0)
    kv2048_prompt = jnp.stack(kv_p[2], axis=0)
    kv128_sample = jnp.stack(kv_s[0], axis=0)
    kv512_sample = jnp.stack(kv_s[1], axis=0)
    kv2048_sample = jnp.stack(kv_s[2], axis=0)
    pool_prompt = jnp.stack(pool_p, axis=0)
    pool_sample = jnp.stack(pool_s, axis=0)
    return (xp, xs, kv128_prompt, kv512_prompt, kv2048_prompt, pool_prompt,
            kv128_sample, kv512_sample, kv2048_sample, pool_sample)
```

```python
import math
from contextlib import ExitStack
import numpy as np
import concourse.bass as bass
import concourse.mybir as mybir
from concourse.bass_utils import run_bass_kernel_spmd

F32 = mybir.dt.float32
BF16 = mybir.dt.bfloat16
ALU = mybir.AluOpType
AF = mybir.ActivationFunctionType
AX = mybir.AxisListType

D = 1024
KC = 8
TP = 2048
TS = 32
T = TP + TS
DFF = 2816
NCH = 22
NEG = -30000.0
EPS = 1e-6
GROUPS = ((128, 1), (512, 4), (2048, 16))
TT = [(0, 512), (512, 512), (1024, 512), (1536, 512), (2048, 32)]
PAIRS = [[0, 1], [2, 3], [4, 5], [6, 7]]
NCLS = (1, 4, 8)
NQ = (8, 2, 1)
WINS = (128, 512, 2048)


def _t5_buckets(distance):
    max_exact = 16
    d = np.asarray(distance, dtype=np.int32)
    ratio = np.log(np.maximum(d, 1).astype(np.float32) / np.float32(max_exact))
    large = max_exact + (ratio / np.float32(math.log(2048 / max_exact)) * (32 - max_exact)).astype(np.int32)
    large = np.minimum(large, 31)
    return np.where(d < max_exact, d, large).astype(np.int32)


def _onehots():
    oh = np.zeros((33, 1200), np.float32)
    for g, (win, dil) in enumerate(GROUPS):
        bk = _t5_buckets(dil * np.arange(129))
        for x in range(384):
            dl = 255 - x
            if 0 <= dl <= 128:
                oh[bk[dl], g * 384 + x] = 1.0
            else:
                oh[32, g * 384 + x] = 1.0
        for x in range(16):
            df = x - 7
            if x < 15 and df >= 0 and df % dil == 0:
                oh[bk[df // dil], 1152 + g * 16 + x] = 1.0
            else:
                oh[32, 1152 + g * 16 + x] = 1.0
    return oh


class _Eng:
    def __init__(self, obj, sem, is_pe=False):
        self.obj = obj
        self.sem = sem
        self.cnt = 0
        self.seen = {}
        self.is_pe = is_pe


class _Slot:
    def __init__(self, sem):
        self.sem = sem
        self.cnt = 0


class _Res:
    __slots__ = ("w", "r")

    def __init__(self):
        self.w = None
        self.r = {}


class Prog:
    def __init__(self, nc, stack):
        self.nc = nc
        self.stack = stack
        self.E = {}
        for name, obj in (("pe", nc.tensor), ("act", nc.scalar), ("dve", nc.vector),
                          ("pool", nc.gpsimd), ("sp", nc.sync)):
            self.E[name] = _Eng(obj, stack.enter_context(nc.semaphore("s_" + name)), name == "pe")
        self.res = {}
        self.slots = []
        self.halt = False

    def slot(self, name):
        s = _Slot(self.stack.enter_context(self.nc.semaphore("d_" + name)))
        self.slots.append(s)
        return s

    def _R(self, k):
        r = self.res.get(k)
        if r is None:
            r = self.res[k] = _Res()
        return r

    def _wait(self, eng, reads, writes):
        deps = {}

        def add(tok, war=False):
            sem, val = tok
            if sem is eng.sem and eng.is_pe:
                return
            k = id(sem)
            if k not in deps or deps[k][1] < val:
                deps[k] = (sem, val)
        for k in reads:
            r = self._R(k)
            if r.w:
                add(r.w)
        for k in writes:
            r = self._R(k)
            if r.w:
                add(r.w)
            for tok in r.r.values():
                add(tok, True)
        for k, (sem, val) in deps.items():
            if eng.seen.get(k, 0) < val:
                eng.obj.wait_ge(sem, val)
                eng.seen[k] = val

    def _mark(self, tok, reads, writes):
        for k in reads:
            self._R(k).r[id(tok[0])] = tok
        for k in writes:
            r = self._R(k)
            r.w = tok
            r.r = {}

    def op(self, en, fn, reads=(), writes=()):
        if self.halt:
            return
        eng = self.E[en]
        self._wait(eng, reads, writes)
        ins = fn(eng.obj)
        eng.cnt += 1
        ins.then_inc(eng.sem, 1)
        self._mark((eng.sem, eng.cnt), reads, writes)

    def dma(self, q, pairs, slot, reads=(), writes=()):
        if self.halt:
            return
        eng = self.E[q]
        self._wait(eng, reads, writes)
        for out, in_ in pairs:
            eng.obj.dma_start(out=out, in_=in_).then_inc(slot.sem, 16)
            slot.cnt += 16
        self._mark((slot.sem, slot.cnt), reads, writes)

    def collective(self, ins_ap, outs_ap, slot, reads=(), writes=()):
        if self.halt:
            return
        eng = self.E["pool"]
        self._wait(eng, reads, writes)
        eng.obj.collective_compute("AllGather", ALU.bypass, replica_groups=PAIRS,
                                   ins=[ins_ap], outs=[outs_ap]).then_inc(slot.sem)
        slot.cnt += 1
        self._mark((slot.sem, slot.cnt), reads, writes)

    def barrier(self):
        toks = [(e.sem, e.cnt) for e in self.E.values() if e.cnt] + [(s.sem, s.cnt) for s in self.slots if s.cnt]
        for e in self.E.values():
            for sem, val in toks:
                if sem is e.sem:
                    continue
                if e.seen.get(id(sem), 0) < val:
                    e.obj.wait_ge(sem, val)
                    e.seen[id(sem)] = val
        self.res = {}


def build_program():
    import os
    STOP = int(os.environ.get("KSTOP", "99"))
    SUB = int(os.environ.get("KSUB", "99"))

    class _Stop(Exception):
        pass

    def chk(n):
        if SUB == n:
            PH[0].halt = True
    PH = [None]
    nc = bass.Bass("TRN2", target_bir_lowering=False)

    def din(name, shape, dt=F32):
        return nc.dram_tensor(name, list(shape), dt, kind="ExternalInput").ap()

    def dout(name, shape):
        return nc.dram_tensor(name, list(shape), F32, kind="ExternalOutput").ap()

    x_tm = din("x_tm", [T, D])
    caches = [din("c128", [4, 128, 512]), din("c512", [4, 512, 512]), din("c2048", [4, 2048, 512])]
    spool = din("spool", [4, 15, 512])
    table = din("table", [32, 12])
    oh_d = din("oh", [33, 1200])
    J_d = din("J", [128, 128])
    id_d = din("ident", [128, 128])
    hmask_d = din("hmask", [128, 1])
    h01_d = din("h01", [128, 1])
    invcnt_d = din("invcnt", [128, 64])
    g_ffn1 = din("g_ffn1", [D])
    g_mix = din("g_mix", [D])
    g_ffn2 = din("g_ffn2", [D])
    wup1 = din("wup1", [D, 2 * DFF])
    wdn1 = din("wdn1", [DFF, D])
    wup2 = din("wup2", [D, 2 * DFF])
    wdn2 = din("wdn2", [DFF, D])
    w_in = din("w_in", [D, 4864])
    qn_d = din("q_norm", [768])
    kn_d = din("k_norm", [768])
    wgrp_d = din("wgrp", [4, 128, 128])
    pscale_d = din("pscale", [512])
    wab_d = din("w_ab", [256, D])
    wpb_d = din("w_pb", [512, D])
    wout_d = din("w_out", [D, D])

    y_tm = dout("y_tm", [T, D])
    kvp = [dout("kvp128", [128, 512]), dout("kvp512", [512, 512]), dout("kvp2048", [2048, 512])]
    poolp = dout("poolp", [15, 512])
    kvs = [dout("kvs128", [4, 128, 512]), dout("kvs512", [4, 512, 512]), dout("kvs2048", [4, 2048, 512])]
    pools = dout("pools", [4, 15, 512])

    ud_t = nc.dram_tensor("ud", [12, 1200], F32)
    ud = ud_t.ap()
    NH = (1, 4, 16)
    bnc, gat = [], []
    for hp in range(2):
        for g in range(3):
            w = NH[g] * 384
            bnc.append(nc.dram_tensor("bnc%d%d" % (hp, g), [128, w], BF16))
            gat.append(nc.dram_tensor("gat%d%d" % (hp, g), [256, w], BF16))
    qsc = [nc.dram_tensor("qsc%d" % hp, [128, TP], BF16) for hp in range(2)]
    bnc_u = nc.dram_tensor("bnc_u", [128, 60], F32)
    gat_u = nc.dram_tensor("gat_u", [256, 60], F32)

    with ExitStack() as st:
        P = Prog(nc, st)
        PH[0] = P

        def sb(name, shape, dt=F32):
            return st.enter_context(nc.sbuf_tensor("t_" + name, list(shape), dt))

        psF = st.enter_context(nc.psum_tensor("psF", [128, 7 * 512], F32))
        psB = st.enter_context(nc.psum_tensor("psB", [128, 1024], BF16))

        def bank(i, n=512):
            return psF[:, i * 512:i * 512 + n]

        xT = sb("xT", [128, KC, T])
        hT = sb("hT", [128, KC, T], BF16)
        attnT = sb("attnT", [128, 2, T], BF16)
        ident = sb("identf", [128, 128])
        identb = sb("identb", [128, 128], BF16)
        onesb = sb("onesb", [128, 128], BF16)
        Jf = sb("Jf", [128, 128])
        gcol = sb("gcol", [128, 3, KC])
        hmask = sb("hmask", [128, 1])
        h01 = sb("h01", [128, 1])
        invcnt = sb("invcnt", [128, 64])
        pscale = sb("pscale", [128, 4])
        biasT = sb("biasT", [128, 3, 4, 2, 128])
        biasS = sb("biasS", [8, 3, 4, 8])

        s_const = P.slot("const")
        P.dma("sp", [(ident[:], id_d[:, :]), (Jf[:], J_d[:, :]), (hmask[:], hmask_d[:, :]),
                     (h01[:], h01_d[:, :]), (invcnt[:], invcnt_d[:, :])], s_const, writes=["const"])
        s_const2 = P.slot("const2")
        s_const3 = P.slot("const3")
        with nc.allow_non_contiguous_dma(reason="tiny one-time gain/scale loads"):
            P.dma("pool", [(gcol[:, 0, :], g_ffn1.rearrange("(c p) -> p c", p=128)),
                         (gcol[:, 1, :], g_mix.rearrange("(c p) -> p c", p=128)),
                         (gcol[:, 2, :], g_ffn2.rearrange("(c p) -> p c", p=128)),
                         (pscale[:], pscale_d.rearrange("(c p) -> p c", p=128))], s_const2, writes=["const2"])
        P.dma("pool", [(identb[:], id_d[:, :])], s_const3, writes=["identb"])
        P.op("dve", lambda e: e.memset(onesb[:], 1.0), writes=["onesb"])

        rr = {"i": 0, "n": 0}

        def copy_op(en, out, in_, reads, writes):
            if en == "act":
                P.op("act", lambda e: e.activation(out=out, in_=in_, func=AF.Copy), reads=reads, writes=writes)
            else:
                P.op(en, lambda e: e.tensor_copy(out=out, in_=in_), reads=reads, writes=writes)

        def alt(*names):
            rr["i"] += 1
            return names[rr["i"] % len(names)]

        def rmsnorm(gi_, bufs=None):
            rr["n"] += 1
            with ExitStack() as ns_:
                if bufs is None:
                    sq = ns_.enter_context(nc.sbuf_tensor("sq%d" % rr["n"], [128, KC, 512], BF16))
                    rstd = ns_.enter_context(nc.sbuf_tensor("rstd%d" % rr["n"], [128, 2, 512], F32))
                else:
                    sq, rstd = bufs
                for ti, (t0, tn) in enumerate(TT):
                    s = ti % 2
                    P.op("act", lambda e: e.activation(out=sq[:, :, 0:tn], in_=xT[:, :, t0:t0 + tn], func=AF.Square),
                         reads=[("xT", ti)], writes=["sq"])
                    for kc in range(KC):
                        P.op("pe", lambda e: e.matmul(bank(6, tn), onesb[:], sq[:, kc, 0:tn], start=(kc == 0), stop=(kc == KC - 1)),
                             reads=["sq", "onesb"], writes=["b6"])
                    P.op("act", lambda e: e.activation(out=rstd[:, s, 0:tn], in_=bank(6, tn), func=AF.Ln, bias=EPS, scale=1.0 / D),
                         reads=["b6"], writes=[("rstd", s)])
                    P.op("act", lambda e: e.activation(out=rstd[:, s, 0:tn], in_=rstd[:, s, 0:tn], func=AF.Exp, scale=-0.5),
                         reads=[("rstd", s)], writes=[("rstd", s)])
                    for kc in range(KC):
                        en = "dve"
                        P.op(en, lambda e: e.scalar_tensor_tensor(out=hT[:, kc, t0:t0 + tn], in0=xT[:, kc, t0:t0 + tn],
                                                                  scalar=gcol[:, gi_, kc:kc + 1], in1=rstd[:, s, 0:tn],
                                                                  op0=ALU.mult, op1=ALU.mult),
                             reads=[("xT", ti), ("rstd", s), "const2"], writes=[("hT", ti)])
                P.barrier()

        cpieces = []
        for g in (2, 1, 0):
            w = WINS[g]
            npc = 4 if g == 2 else 1
            rows = (w - 8) // npc
            for bb in range(4):
                for pc in range(npc):
                    r0 = pc * rows
                    cpieces.append((kvs[g][bb, r0:r0 + rows, :], caches[g][bb, 8 + r0:8 + r0 + rows, :]))
        s_cc = P.slot("ccopy")

        def copy_piece(dep):
            if cpieces:
                d_, s_ = cpieces.pop(0)
                P.dma("sp", [(d_, s_)], s_cc, reads=[dep])

        def ffn(wup, wdn, tag, norm_idx):
            HC = NCH // 2
            with nc.sbuf_tensor("wupb" + tag, [128, 2, KC, 2, 256], BF16) as wupb:
              su = [P.slot("wup0" + tag), P.slot("wup1" + tag)]

              def load_up(hf, c0, s):
                  ncl = min(2, HC - c0) * 128
                  col = (hf * HC + c0) * 128
                  P.dma("pool", [(wupb[:, s, :, 0, 0:ncl], wup[:, col:col + ncl].rearrange("(kc p) c -> p kc c", p=128)),
                                 (wupb[:, s, :, 1, 0:ncl], wup[:, DFF + col:DFF + col + ncl].rearrange("(kc p) c -> p kc c", p=128))],
                        su[s], writes=[("wup", s)])
              load_up(0, 0, 0)
              rmsnorm(norm_idx)
              with nc.sbuf_tensor("actT" + tag, [128, HC, T], BF16) as actT, \
                    nc.sbuf_tensor("wdnb" + tag, [128, 2, HC, 256], BF16) as wdnb, \
                    nc.sbuf_tensor("silu" + tag, [128, 2, 512], F32) as silu:
                sd = [P.slot("wdn0" + tag), P.slot("wdn1" + tag)]
                nb = 0
                nd = 0
                it = 0
                for hf in range(2):
                    for c0 in range(0, HC, 2):
                        ncl = min(2, HC - c0) * 128
                        s = nb % 2
                        nb += 1
                        if not (hf == 0 and c0 == 0):
                            load_up(hf, c0, s)
                        for cc in range(ncl // 128):
                            c = c0 + cc
                            for ti, (t0, tn) in enumerate(TT):
                                b = it % 2
                                it += 1
                                for kc in range(KC):
                                    P.op("pe", lambda e: e.matmul(bank(b, tn), wupb[:, s, kc, 0, cc * 128:(cc + 1) * 128], hT[:, kc, t0:t0 + tn],
                                                                  start=(kc == 0), stop=(kc == KC - 1)),
                                         reads=[("wup", s), ("hT", ti)], writes=[("bk", b)])
                                for kc in range(KC):
                                    P.op("pe", lambda e: e.matmul(bank(2 + b, tn), wupb[:, s, kc, 1, cc * 128:(cc + 1) * 128], hT[:, kc, t0:t0 + tn],
                                                                  start=(kc == 0), stop=(kc == KC - 1)),
                                         reads=[("wup", s), ("hT", ti)], writes=[("bk", 2 + b)])
                                P.op("act", lambda e: e.activation(out=silu[:, b, 0:tn], in_=bank(b, tn), func=AF.Silu),
                                     reads=[("bk", b)], writes=[("silu", b)])
                                P.op("dve", lambda e: e.tensor_tensor(out=actT[:, c, t0:t0 + tn], in0=bank(2 + b, tn), in1=silu[:, b, 0:tn], op=ALU.mult),
                                     reads=[("bk", 2 + b), ("silu", b)], writes=[("actT", c, ti)])
                        if nb % 2 == 0:
                            copy_piece(("bk", b))
                    for f0 in range(0, KC, 2):
                        s = nd % 2
                        nd += 1
                        P.dma("pool", [(wdnb[:, s, :, :], wdn[hf * HC * 128:(hf + 1) * HC * 128, f0 * 128:(f0 + 2) * 128].rearrange("(c p) f -> p c f", p=128))],
                              sd[s], writes=[("wdn", s)])
                        for ff in range(2):
                            fc = f0 + ff
                            for ti, (t0, tn) in enumerate(TT):
                                b = 4 + it % 2
                                it += 1
                                for c in range(HC):
                                    P.op("pe", lambda e: e.matmul(bank(b, tn), wdnb[:, s, c, ff * 128:(ff + 1) * 128], actT[:, c, t0:t0 + tn],
                                                                  start=(c == 0), stop=(c == HC - 1)),
                                         reads=[("wdn", s), ("actT", c, ti)], writes=[("bk", b)])
                                P.op("dve", lambda e: e.scalar_tensor_tensor(out=xT[:, fc, t0:t0 + tn], in0=bank(b, tn), scalar=0.5,
                                                                             in1=xT[:, fc, t0:t0 + tn], op0=ALU.mult, op1=ALU.add),
                                     reads=[("bk", b), ("xT", ti)], writes=[("xT", ti)])
                P.barrier()


        bstack = ExitStack()
        tab = bstack.enter_context(nc.sbuf_tensor("t_tabaug", [33, 12], F32))
        ohs = bstack.enter_context(nc.sbuf_tensor("t_ohs", [33, 1200], F32))
        uds = bstack.enter_context(nc.sbuf_tensor("t_uds", [12, 1200], F32))
        Tp = bstack.enter_context(nc.sbuf_tensor("t_Tp", [128, 12, 256], F32))
        Bp = bstack.enter_context(nc.sbuf_tensor("t_Bp", [8, 12, 8], F32))
        s_b = P.slot("bias")
        s_b2 = P.slot("bias2")
        s_b3 = P.slot("bias3")
        s_tp = P.slot("tp")
        P.op("dve", lambda e: e.memset(tab[:], NEG), writes=["tab"])
        P.dma("pool", [(tab[0:32, :], table[:, :])], s_b, writes=["tab"])
        P.dma("pool", [(ohs[:], oh_d[:, :])], s_b2, writes=["ohs"])
        for (c0, cn) in ((0, 512), (512, 512), (1024, 176)):
            P.op("pe", lambda e: e.matmul(psF[0:12, c0:c0 + cn], tab[:, :], ohs[:, c0:c0 + cn], start=True, stop=True),
                 reads=["tab", "ohs"], writes=[("tb", 0), ("tb", 1)])
        P.op("dve", lambda e: e.tensor_copy(out=uds[:], in_=psF[0:12, 0:1200]), reads=[], writes=["uds", ("tb", 0), ("tb", 1)])
        P.dma("pool", [(ud[:, :], uds[:])], s_b3, reads=["uds"], writes=["ud"])
        prs = []
        for g in range(3):
            for h in range(4):
                k = g * 4 + h
                prs.append((Tp[:, k, :], bass.AP(ud_t, (4 * g + h) * 1200 + g * 384, [[1, 128], [1, 256]])))
                prs.append((Bp[:, k, :], bass.AP(ud_t, (4 * g + h) * 1200 + 1152 + g * 16, [[1, 8], [1, 8]])))
        P.dma("pool", prs, s_tp, reads=["ud"], writes=["Tp"])

        def bias_setup_2():
            for g in range(3):
                for h in range(4):
                    k = g * 4 + h
                    s = k % 2
                    for kb in range(2):
                        P.op("pe", lambda e: e.matmul(bank(4 + s, 256)[:, kb * 128:(kb + 1) * 128], Tp[:, k, kb * 128:(kb + 1) * 128], Jf[:, :],
                                                      start=True, stop=True), reads=["Tp", "const"], writes=[("bk", 4 + s)])
                    P.op("act", lambda e: e.activation(out=biasT[:, g, h, :, :], in_=bank(4 + s, 256).rearrange("p (a b) -> p a b", b=128), func=AF.Copy),
                         reads=[("bk", 4 + s)], writes=["biasT"])
                    P.op("pe", lambda e: e.matmul(bank(6, 8)[0:8, :], Jf[0:8, 120:128], Bp[:, k, :], start=True, stop=True),
                         reads=["Tp", "const"], writes=["b6"])
                    P.op("dve", lambda e: e.tensor_copy(out=biasS[:, g, h, :], in_=bank(6, 8)[0:8, :]), reads=["b6"], writes=["biasS"])
            P.barrier()
            bstack.close()

        with nc.sbuf_tensor("xin", [128, 4, D], F32) as xin:
            sx = [P.slot("xin%d" % i) for i in range(4)]
            for j in range(17):
                n = 128 if j < 16 else TS
                s = j % 2
                s4 = j % 4
                P.dma("sp", [(xin[0:n, s4, :], x_tm[j * 128:j * 128 + n, :])], sx[s4], writes=[("xin", s4)])
                for kc in range(KC):
                    P.op("pe", lambda e: e.transpose(psF[:, s * 1024 + kc * 128:s * 1024 + kc * 128 + n],
                                                     xin[0:n, s4, kc * 128:(kc + 1) * 128], ident[0:n, 0:n]),
                         reads=[("xin", s4), "const"], writes=[("tb", s)])
                copy_op(alt("act", "dve"), xT[:, :, j * 128:j * 128 + n],
                        psF[:, s * 1024:(s + 1) * 1024].rearrange("p (k t) -> p k t", t=128)[:, :, 0:n],
                        [("tb", s)], [("xT", j // 4)])
            P.barrier()

        bias_setup_2()
        ffn(wup1, wdn1, "1", 0)

        def attention():
            with ExitStack() as a:
                def sba(name, shape, dt=F32):
                    return a.enter_context(nc.sbuf_tensor("t_" + name, list(shape), dt))
                NSP = 4
                PQB = [0, 1, 4, 5]
                NSA = 3
                SB = [2, 3, 6]
                gqk = sba("gqk", [128, 2, 128])
                Vaug = sba("Vaug", [128, 16, 2, 128], BF16)
                Vs = sba("Vs", [128, 2, 128], BF16)
                Vh = sba("Vh", [128, 16, 2, 128], BF16)
                kTh = sba("kTh", [128, 16 * 128], BF16)
                kT = sba("kT", [128, TP], BF16)
                qTm = sba("qTm", [128, 2, TP], BF16)
                kTs = sba("kTs", [128, 4, 8], BF16)
                qTms = sba("qTms", [128, 4, 2, 8], BF16)
                acc = sba("acc", [128, 2, TP])
                accs = sba("accs", [128, 2, 32])
                wq = sba("wq", [128, KC, 3, 128], BF16)
                buf = sba("pbuf", [128, NSP, 256])
                ssm = sba("ssm", [128, NSP, 4])
                vst = sba("vst", [128, NSP, 128])
                qkb = sba("qkb", [128, NSP, 256], BF16)
                sE = sba("sE", [128, NSA, 512])
                Pt = sba("Pt", [128, NSA, 512], BF16)
                rec = sE
                sEs = sba("sEs", [128, 2, 32])
                Pts = sba("Pts", [128, 2, 32], BF16)
                sEc = sba("sEc", [32, 2, 8])
                biasS32 = sba("biasS32", [32, 4, 3, 4, 8])
                Pcs = sba("Pcs", [128, 2, 8], BF16)
                ck = sba("ck", [128, 3, 4, 128], BF16)
                Vc = sba("Vc", [128, 3, 4, 2, 128], BF16)
                kTc = sba("kTc", [128, 3, 4 * 128], BF16)

                def late_memsets():
                    P.op("pool", lambda e: e.memset(Vaug[:], 1.0), writes=["Vaug"])
                    P.op("pool", lambda e: e.memset(Vs[:], 1.0), writes=["Vaug"])
                lm = [late_memsets]
                P.op("pool", lambda e: e.memset(biasS32[:], NEG), writes=["biasS32"])
                s_bs = P.slot("biasS32")
                P.dma("sp", [(biasS32[8 * bb:8 * bb + 8, bb, :, :, :].rearrange("p g h q -> p (g h q)"),
                              biasS[:, :, :, :].rearrange("p g h q -> p (g h q)")) for bb in range(4)],
                      s_bs, reads=["biasS"], writes=["biasS32"])
                P.op("pool", lambda e: e.memset(Vc[:], 1.0), writes=[("Vc", 0), ("Vc", 1), ("Vc", 2)])
                P.op("dve", lambda e: e.memset(qTm[:], 0.0), writes=["qTm"])
                P.op("dve", lambda e: e.memset(qTms[:], 0.0), writes=["qTm"])
                P.op("dve", lambda e: e.memset(Pcs[:], 0.0), writes=["Pcs"])

                s_w = P.slot("wq")
                s_g = P.slot("gqk")
                s_k = [P.slot("kst%d" % i) for i in range(NSP)]
                s_v = [P.slot("vst%d" % i) for i in range(NSP)]
                s_x = P.slot("xchg")
                s_cx = P.slot("cc")
                s_h = P.slot("halo")
                s_c = [P.slot("ck0"), P.slot("ck1"), P.slot("ck2")]

                def proj_sa(s, M, lhs_cols, vdst):
                    pq = bank(PQB[s], 384)
                    bk = ("bk", PQB[s])
                    for kc in range(KC):
                        P.op("pe", lambda e: e.matmul(pq[0:M, :], lhs_cols(kc), wq[:, kc, :, :].rearrange("p a b -> p (a b)"),
                                                      start=(kc == 0), stop=(kc == KC - 1)),
                             reads=["wq"], writes=[bk])
                    P.op("act", lambda e: e.activation(out=buf[0:M, s, :], in_=pq[0:M, 0:256], func=AF.Square),
                         reads=[], writes=[("buf", s), bk])
                    P.op("act", lambda e: e.activation(out=vst[0:M, s, :], in_=pq[0:M, 256:384], func=AF.Copy),
                         reads=[], writes=[("vst", s), bk])
                    P.op("dve", lambda e: e.tensor_copy(out=vdst, in_=pq[0:M, 256:384].rearrange("p (h d) -> p h d", h=2)),
                         reads=[], writes=["Vaug", bk])
                    P.op("dve", lambda e: e.tensor_reduce(out=ssm[0:M, s, :], in_=buf[0:M, s, :].rearrange("p (a b) -> p a b", b=64),
                                                          axis=AX.X, op=ALU.add), reads=[("buf", s)], writes=[("ssm", s)])

                def proj_sb(s, M):
                    pq = bank(PQB[s], 384)
                    bk = ("bk", PQB[s])
                    P.op("act", lambda e: e.activation(out=ssm[0:M, s, :], in_=ssm[0:M, s, :], func=AF.Sqrt, bias=EPS, scale=1.0 / 64),
                         reads=[("ssm", s)], writes=[("ssm", s)])
                    P.op("dve", lambda e: e.reciprocal(out=ssm[0:M, s, :], in_=ssm[0:M, s, :]), reads=[("ssm", s)], writes=[("ssm", s)])
                    P.op("dve", lambda e: e.tensor_tensor(out=buf[0:M, s, :].rearrange("p (a b) -> p a b", b=64),
                                                          in0=pq[0:M, 0:256].rearrange("p (a b) -> p a b", b=64),
                                                          in1=ssm[0:M, s, :].unsqueeze(2).to_broadcast([M, 4, 64]), op=ALU.mult),
                         reads=[("ssm", s)], writes=[("buf", s), bk])

                def proj_sc(s, M):
                    P.op("pool", lambda e: e.tensor_tensor(out=buf[0:M, s, :], in0=buf[0:M, s, :], in1=gqk[0:M, :, :].rearrange("p a b -> p (a b)"), op=ALU.mult),
                         reads=[("buf", s), "gqk"], writes=[("buf", s)])
                    P.op("pool", lambda e: e.tensor_copy(out=qkb[0:M, s, :], in_=buf[0:M, s, :]),
                         reads=[("buf", s)], writes=[("qkb", s)])

                def proj_B(s, M, q0dst, q1dst, kdst, dmas, view=None):
                    for w2 in range(2):
                        P.op("pe", lambda e: e.transpose(psB[:, w2 * 128:w2 * 128 + M], qkb[0:M, s, w2 * 128:(w2 + 1) * 128], identb[0:M, 0:M]),
                             reads=[("qkb", s), "identb"], writes=["pB"])
                    vw = view if view is not None else (lambda a: a)
                    P.op("act", lambda e: e.activation(out=q0dst, in_=vw(psB[0:64, 0:M]), func=AF.Copy), reads=[], writes=["qTm", "pB"])
                    P.op("act", lambda e: e.activation(out=q1dst, in_=vw(psB[64:128, 0:M]), func=AF.Copy), reads=[], writes=["qTm", "pB"])
                    P.op("dve", lambda e: e.tensor_copy(out=kdst, in_=vw(psB[:, 128:128 + M])), reads=[], writes=["kT", "pB"])
                    for (dk, dv, p0, pn) in dmas:
                        P.dma("sp", [(dk, buf[p0:p0 + pn, s, 128:256])], s_k[s], reads=[("buf", s)])
                        P.dma("sp", [(dv, vst[p0:p0 + pn, s, :])], s_v[s], reads=[("vst", s)])

                def blk_S(i, g, hp, kTp, kTcur, qc, halo):
                    s = i % NSA
                    pS = bank(SB[s], 512).rearrange("p (h k q) -> p h k q", h=2, k=2)
                    for h in range(2):
                        P.op("pe", lambda e: e.matmul(pS[:, h, 0, :], kTp, qTm[:, h, qc], start=True, stop=True),
                             reads=["kT", "qTm"] + (["halo"] if halo else []), writes=[("bk", SB[s])])
                        P.op("pe", lambda e: e.matmul(pS[:, h, 1, :], kTcur, qTm[:, h, qc], start=True, stop=True),
                             reads=["kT", "qTm"], writes=[("bk", SB[s])])
                    P.op("dve", lambda e: e.tensor_tensor(out=sE[:, s, :], in0=bank(SB[s], 512),
                                                          in1=biasT[:, g, 2 * hp:2 * hp + 2, :, :].rearrange("p h k q -> p (h k q)"), op=ALU.add),
                         reads=[("bk", SB[s]), "biasT"], writes=[("sE", s)])
                    sE4 = sE[:, s, :].rearrange("p (h k q) -> p h k q", h=2, k=2)
                    Pt4 = Pt[:, s, :].rearrange("p (h k q) -> p h k q", h=2, k=2)
                    if halo:
                        P.op("act", lambda e: e.activation(out=Pt4[:, :, 0, :], in_=sE4[:, :, 0, :], func=AF.Exp, bias=hmask[:, 0:1]),
                             reads=[("sE", s), "const"], writes=[("Pt", s)])
                        P.op("act", lambda e: e.activation(out=Pt4[:, :, 1, :], in_=sE4[:, :, 1, :], func=AF.Exp),
                             reads=[("sE", s)], writes=[("Pt", s)])
                    else:
                        P.op("act", lambda e: e.activation(out=Pt[:, s, :], in_=sE[:, s, :], func=AF.Exp),
                             reads=[("sE", s)], writes=[("Pt", s)])

                def blk_V(i, first, Vp, Vcur, acc_view, halo):
                    s = i % NSA
                    b = 4 + i % 2
                    Pt4 = Pt[:, s, :].rearrange("p (h k q) -> p h k q", h=2, k=2)
                    pO = bank(b, 256).rearrange("p (h q) -> p h q", h=2)
                    for h in range(2):
                        P.op("pe", lambda e: e.matmul(pO[:, h, :], Vp(h), Pt4[:, h, 0, :], start=True, stop=False),
                             reads=[("Pt", s), "Vaug"] + (["halo"] if halo else []), writes=[("bk", b)])
                        P.op("pe", lambda e: e.matmul(pO[:, h, :], Vcur(h), Pt4[:, h, 1, :], start=False, stop=True),
                             reads=[("Pt", s), "Vaug"], writes=[("bk", b)])
                    if first:
                        P.op("dve", lambda e: e.tensor_copy(out=acc_view, in_=pO), reads=[("bk", b)], writes=["acc"])
                    else:
                        P.op("dve", lambda e: e.tensor_tensor(out=acc_view, in0=pO, in1=acc_view, op=ALU.add),
                             reads=[("bk", b), "acc"], writes=["acc"])

                def run_blocks(hp, g, dil, nb, blocks, cnt):
                    pend = []
                    for (c, b) in blocks:
                        j = c * nb + b
                        tstart = c + dil * 128 * b
                        tok = slice(tstart, tstart + 127 * dil + 1, dil) if dil > 1 else slice(tstart, tstart + 128)
                        halo = (b == 0)
                        if halo:
                            kTp = kTh[:, c * 128:(c + 1) * 128]
                            Vp = (lambda h, c=c: Vh[:, c, h, :])
                        else:
                            kTp = kT[:, (j - 1) * 128:j * 128]
                            Vp = (lambda h, j=j: Vaug[:, j - 1, h, :])
                        i = cnt[0]
                        cnt[0] += 1
                        blk_S(i, g, hp, kTp, kT[:, j * 128:(j + 1) * 128], slice(j * 128, (j + 1) * 128), halo)
                        pend.append((i, Vp, (lambda h, j=j: Vaug[:, j, h, :]), acc[:, :, tok], halo))
                        if len(pend) > 2:
                            pi_, Vp_, Vc_, av_, hl_ = pend.pop(0)
                            blk_V(pi_, g == 0, Vp_, Vc_, av_, hl_)
                    for (pi_, Vp_, Vc_, av_, hl_) in pend:
                        blk_V(pi_, g == 0, Vp_, Vc_, av_, hl_)

                bcnt = [0]
                s_r = P.slot("reload")
                s_q = P.slot("qsc")
                c0f = 2 * 256
                P.dma("pool", [(wq[:, :, w3, :], w_in[:, w3 * 768 + c0f:w3 * 768 + c0f + 128].rearrange("(kc p) c -> p kc c", p=128))
                               for w3 in range(3)], s_w, writes=["wq"])
                P.dma("sp", [(gqk[:, 0, :], bass.AP(qn_d.tensor, c0f, [[0, 128], [1, 128]])),
                             (gqk[:, 1, :], bass.AP(kn_d.tensor, c0f, [[0, 128], [1, 128]]))], s_g, writes=["gqk"])
                P.op("dve", lambda e: e.tensor_scalar(out=gqk[:, 0, :], in0=gqk[:, 0, :], scalar1=0.125, scalar2=None, op0=ALU.mult),
                     reads=["gqk"], writes=["gqk"])
                lm[0]()
                lm[0] = None
                rmsnorm(1, bufs=(Vh[:, :, :, :].rearrange("p j h d -> p (j h d)").rearrange("p (k t) -> p k t", k=KC), sE[:, 0:2, :]))
                first_pass = [True]
                for hp in range(2):
                    for g, part in ((2, "AC"), (0, "AHBCD"), (1, "AHBCD"), (2, "HRD")):
                        win, dil = GROUPS[g]
                        nb = TP // (128 * dil)
                        ncl = dil
                        pi = hp * 3 + g
                        c0 = g * 256 + hp * 128
                        nh = NH[g]
                        bt, gt = bnc[pi], gat[pi]
                        ncs, nq = NCLS[g], NQ[g]
                        nch = (ncs + 3) // 4
                        ncu = min(ncs, 4)
                        units = [(bb, ch) for bb in range(4) for ch in range(nch)]

                        def unit_dma(u):
                            bb, ch = units[u]
                            s = u % 3
                            cl0 = ch * 4
                            ksrc = bass.AP(caches[g].tensor, bb * win * 512 + cl0 * 512 + hp * 128, [[dil * 512, 128], [512, ncu], [1, 128]])
                            vsrc = [bass.AP(caches[g].tensor, bb * win * 512 + cl0 * 512 + 256 + hp * 128 + h * 64, [[dil * 512, 128], [512, ncu], [1, 64]]) for h in range(2)]
                            P.dma("pool", [(ck[:, s, 0:ncu, :], ksrc), (Vc[:, s, 0:ncu, 0, 0:64], vsrc[0]), (Vc[:, s, 0:ncu, 1, 0:64], vsrc[1])],
                                  s_c[s], writes=[("ck", s), ("Vc", s)])

                        if "A" in part:
                            if not first_pass[0]:
                                P.dma("pool", [(wq[:, :, w3, :], w_in[:, w3 * 768 + c0:w3 * 768 + c0 + 128].rearrange("(kc p) c -> p kc c", p=128))
                                               for w3 in range(3)], s_w, writes=["wq"])
                            for u in range(3):
                                unit_dma(u)
                            if not first_pass[0]:
                                P.dma("sp", [(gqk[:, 0, :], bass.AP(qn_d.tensor, c0, [[0, 128], [1, 128]])),
                                             (gqk[:, 1, :], bass.AP(kn_d.tensor, c0, [[0, 128], [1, 128]]))], s_g, writes=["gqk"])
                                P.op("dve", lambda e: e.tensor_scalar(out=gqk[:, 0, :], in0=gqk[:, 0, :], scalar1=0.125, scalar2=None, op0=ALU.mult),
                                     reads=["gqk"], writes=["gqk"])
                            first_pass[0] = False
                            items = []
                            for c in range(ncl):
                                for b in range(nb):
                                    j = c * nb + b
                                    tstart = c + dil * 128 * b
                                    tok = slice(tstart, tstart + 127 * dil + 1, dil) if dil > 1 else slice(tstart, tstart + 128)
                                    dmas = []
                                    if b == nb - 1:
                                        dmas = [(bass.AP(kvp[g].tensor, c * 512 + hp * 128, [[dil * 512, 128], [1, 128]]),
                                                 bass.AP(kvp[g].tensor, c * 512 + 256 + hp * 128, [[dil * 512, 128], [1, 128]]), 0, 128)]
                                    items.append(dict(M=128, lhs=(lambda kc, tok=tok: hT[:, kc, tok]), vdst=Vaug[:, j, :, 0:64],
                                                      q0=qTm[0:64, 0, j * 128:(j + 1) * 128], q1=qTm[64:128, 1, j * 128:(j + 1) * 128],
                                                      k=kT[:, j * 128:(j + 1) * 128], dmas=dmas, halo=(b == nb - 1)))
                            items = [i_ for i_ in items if i_["halo"]] + [i_ for i_ in items if not i_["halo"]]
                            n_h = ncl
                            items.append(dict(M=32, lhs=(lambda kc: hT[:, kc, TP:T]), vdst=Vs[0:32, :, 0:64],
                                              q0=qTms[0:64, :, 0, :], q1=qTms[64:128, :, 1, :], k=kTs[:, :, :],
                                              view=(lambda a: a.rearrange("p (b t) -> p b t", b=4)),
                                              dmas=[(kvs[g][bb, win - 8:win, hp * 128:(hp + 1) * 128],
                                                     kvs[g][bb, win - 8:win, 256 + hp * 128:256 + (hp + 1) * 128], 8 * bb, 8) for bb in range(4)]))
                            n_it = len(items)
                            for it_ in range(n_it + 3):
                                if it_ < n_it:
                                    proj_sa(it_ % NSP, items[it_]["M"], items[it_]["lhs"], items[it_]["vdst"])
                                if 0 <= it_ - 1 < n_it:
                                    proj_sb((it_ - 1) % NSP, items[it_ - 1]["M"])
                                if 0 <= it_ - 2 < n_it:
                                    proj_sc((it_ - 2) % NSP, items[it_ - 2]["M"])
                                if 0 <= it_ - 3 < n_it:
                                    o = items[it_ - 3]
                                    proj_B((it_ - 3) % NSP, o["M"], o["q0"], o["q1"], o["k"], o["dmas"], o.get("view"))
                                if it_ == n_h + 2:
                                    P.dma("sp", [(bt.ap()[:, 0:nh * 128].rearrange("p (c t) -> p c t", c=ncl),
                                                  kT[:, :].rearrange("p (c t) -> p c t", c=ncl)[:, :, (nb - 1) * 128:nb * 128]),
                                                 (bt.ap()[:, nh * 128:nh * 384].rearrange("p (c x) -> p c x", c=ncl),
                                                  Vaug[:, :, :, :].rearrange("p (c b) h d -> p c b (h d)", c=ncl)[:, :, nb - 1, :])],
                                          s_x, reads=["kT", "Vaug"], writes=["bnc"])
                                    P.collective(bt.ap().opt(), gt.ap().opt(), s_cx, reads=["bnc"], writes=["gat"])
                                    if g == 2:
                                        P.dma("sp", [(qsc[hp].ap()[0:64, :], qTm[0:64, 0, :]), (qsc[hp].ap()[64:128, :], qTm[64:128, 1, :])],
                                              s_q, reads=["qTm"], writes=["qsc"])
                        if "H" in part:
                            P.dma("sp", [(kTh[:, 0:nh * 128], gt.ap()[0:128, 0:nh * 128]),
                                           (Vh[:, 0:nh, :, :].rearrange("p j h d -> p (j h d)"), gt.ap()[0:128, nh * 128:nh * 384])],
                                  s_h, reads=["gat"], writes=["halo"])
                        if "R" in part:
                            P.dma("sp", [(kT[:, :], bt.ap()[:, 0:TP]),
                                         (Vaug[:, :, :, :].rearrange("p j h d -> p (j h d)"), bt.ap()[:, TP:3 * TP]),
                                         (qTm[0:64, 0, :], qsc[hp].ap()[0:64, :]), (qTm[64:128, 1, :], qsc[hp].ap()[64:128, :])],
                                  s_r, reads=["bnc", "qsc"], writes=["kT", "Vaug", "qTm"])
                        if "B" in part:
                            run_blocks(hp, g, dil, nb, [(c, b) for c in range(ncl) for b in range(1, nb)], bcnt)
                        if "C" in part:

                            def unit_T(u, bb, ch):
                                s = u % 3
                                for c in range(ncu):
                                    P.op("pe", lambda e: e.transpose(psB[:, c * 128:(c + 1) * 128], ck[:, s, c, :], identb[:, :]),
                                         reads=[("ck", s), "identb"], writes=["pB"])
                                P.op("dve", lambda e: e.tensor_copy(out=kTc[:, s, 0:ncu * 128], in_=psB[:, 0:ncu * 128]),
                                     reads=[], writes=[("kTc", s), "pB"])

                            def unit_S(u, bb, ch):
                                s = u % 2
                                s3 = u % 3
                                cl0 = ch * 4
                                pSp = bank(0, 2 * ncu * nq).rearrange("p (h c q) -> p h c q", h=2, c=ncu)
                                pSc = bank(0, 512)[0:32, 64:80].rearrange("p (h q) -> p h q", h=2)
                                for h in range(2):
                                    if ch == 0:
                                        P.op("pe", lambda e: e.matmul(pSc[:, h, :], kTs[:, :, :].rearrange("p b t -> p (b t)"), qTms[:, bb, h, :], start=True, stop=True),
                                             reads=["kT", "qTm"], writes=[("bk", 0)])
                                    for c in range(ncu):
                                        cl = cl0 + c
                                        P.op("pe", lambda e: e.matmul(pSp[:, h, c, :], kTc[:, s3, c * 128:(c + 1) * 128], qTms[:, bb, h, cl:8:dil],
                                                                      start=True, stop=True), reads=["qTm", ("kTc", s3)], writes=[("bk", 0)])
                                sEp = sEs[:, s, 0:2 * ncu * nq].rearrange("p (h c q) -> p h c q", h=2, c=ncu)
                                Pp = Pts[:, s, 0:2 * ncu * nq].rearrange("p (h c q) -> p h c q", h=2, c=ncu)
                                bsrc = biasT[:, g, 2 * hp:2 * hp + 2, 0, 0:nq].unsqueeze(2).to_broadcast([128, 2, ncu, nq])
                                P.op("dve", lambda e: e.tensor_tensor(out=sEp, in0=pSp, in1=bsrc, op=ALU.add),
                                     reads=[("bk", 0), "biasT"], writes=[("sEs", s)])
                                P.op("act", lambda e: e.activation(out=Pp, in_=sEp, func=AF.Exp), reads=[("sEs", s)], writes=[("Pts", s)])
                                if ch == 0:
                                    P.op("dve", lambda e: e.tensor_tensor(out=sEc[:], in0=pSc, in1=biasS32[:, bb, g, 2 * hp:2 * hp + 2, :], op=ALU.add),
                                         reads=[("bk", 0), "biasS32"], writes=["sEc"])
                                    P.op("act", lambda e: e.activation(out=Pcs[0:32, :, :], in_=sEc[:], func=AF.Exp), reads=["sEc"], writes=["Pcs"])

                            def unit_V(u, bb, ch):
                                s = u % 2
                                s3 = u % 3
                                cl0 = ch * 4
                                Pp = Pts[:, s, 0:2 * ncu * nq].rearrange("p (h c q) -> p h c q", h=2, c=ncu)
                                pOs = psF[:, 2048:3072].rearrange("p (h x) -> p h x", h=2)[:, :, 0:8]
                                for h in range(2):
                                    if ch == 0:
                                        P.op("pe", lambda e: e.matmul(pOs[:, h, :], Vs[:, h, :], Pcs[:, h, :], start=True, stop=False),
                                             reads=["Pcs", "Vaug"], writes=[("bk", 4), ("bk", 5)])
                                    for c in range(ncu):
                                        cl = cl0 + c
                                        P.op("pe", lambda e: e.matmul(pOs[:, h, cl:8:dil], Vc[:, s3, c, h, :], Pp[:, h, c, :], start=False,
                                                                      stop=(ch == nch - 1 and c == ncu - 1)),
                                             reads=[("Pts", s), ("Vc", s3)], writes=[("bk", 4), ("bk", 5)])
                                if ch == nch - 1:
                                    av = accs[:, :, bb * 8:(bb + 1) * 8]
                                    if g == 2:
                                        P.op("dve", lambda e: e.tensor_copy(out=av, in_=pOs), reads=[("bk", 4), ("bk", 5)], writes=["accs"])
                                    else:
                                        P.op("dve", lambda e: e.tensor_tensor(out=av, in0=pOs, in1=av, op=ALU.add), reads=[("bk", 4), ("bk", 5), "accs"], writes=["accs"])

                            for u in range(min(2, len(units))):
                                unit_T(u, *units[u])
                            for u, (bb, ch) in enumerate(units):
                                unit_S(u, bb, ch)
                                if u + 2 < len(units):
                                    unit_T(u + 2, *units[u + 2])
                                unit_V(u, bb, ch)
                                if u + 3 < len(units):
                                    unit_dma(u + 3)
                        if "D" in part:
                            run_blocks(hp, g, dil, nb, [(c, 0) for c in range(ncl)], bcnt)
                    for q0 in range(0, TP, 512):
                        for h in range(2):
                            P.op("act", lambda e: e.activation(out=rec[0:64, h, :], in_=acc[64:128, h, q0:q0 + 512], func=AF.Ln),
                                 reads=["acc"], writes=[("sE", h)])
                            P.op("act", lambda e: e.activation(out=rec[0:64, h, :], in_=rec[0:64, h, :], func=AF.Exp, scale=-1.0),
                                 reads=[("sE", h)], writes=[("sE", h)])
                        for h in range(2):
                            P.op("dve", lambda e: e.tensor_tensor(out=attnT[h * 64:(h + 1) * 64, hp, q0:q0 + 512], in0=acc[0:64, h, q0:q0 + 512],
                                                                  in1=rec[0:64, h, :], op=ALU.mult), reads=["acc", ("sE", h)], writes=["attnT"])
                    P.op("dve", lambda e: e.reciprocal(out=rec[0:64, 2, 0:64].rearrange("p (h q) -> p h q", h=2), in_=accs[64:128, :, :]),
                         reads=["accs"], writes=[("sE", 2)])
                    for h in range(2):
                        P.op("dve", lambda e: e.tensor_tensor(out=attnT[h * 64:(h + 1) * 64, hp, TP:T], in0=accs[0:64, h, :],
                                                              in1=rec[0:64, 2, h * 32:(h + 1) * 32], op=ALU.mult), reads=["accs", ("sE", 2)], writes=["attnT"])
                P.barrier()

        if STOP >= 4:
            attention()
            P.halt = False
            P.barrier()

        def p4():
          with nc.sbuf_tensor("pooledT", [128, 4, T], BF16) as pooledT:
              with ExitStack() as a:
                  def sba(name, shape, dt=F32):
                      return a.enter_context(nc.sbuf_tensor("t_" + name, list(shape), dt))
                  uT = sba("uT", [128, 4, 16 + TP])
                  us = sba("us", [128, 4, 4, 24])
                  dT = sba("dT", [128, 4, T], BF16)
                  wu = sba("wu", [128, KC, 512], BF16)
                  wg = sba("wg", [128, 4, 128], BF16)
                  t1 = sba("t1", [128, 528])
                  t2 = sba("t2", [128, 528])
                  pre = sba("pre", [128, 4, 15])
                  stt = sba("stt", [16, 1, 512])
                  pout = sba("pout", [16, 512])
                  pouts = sba("pouts", [8, 1, 512])
                  d16 = sba("d16", [128, 16])
                  s_p = P.slot("pool")
                  s_st = [P.slot("stt0"), P.slot("stt1")]
                  s_p3 = P.slot("pool3")
                  s_p4 = P.slot("pool4")
                  s_po = P.slot("poolout")
                  s_ps = [P.slot("pouts0"), P.slot("pouts1")]
                  P.dma("pool", [(wu[:], w_in[:, 2304:2816].rearrange("(kc p) c -> p kc c", p=128)),
                                 (wg[:], wgrp_d.rearrange("g c d -> c g d"))], s_p, writes=["wu"])
                  P.dma("sp", [(pools[bb, 0:7, :], spool[bb, 8:15, :]) for bb in range(4)], s_po)
                  P.op("pool", lambda e: e.memset(uT[:, :, 0:16], 0.0), writes=["uTpre"])
                  P.op("pool", lambda e: e.memset(us[:], 0.0), writes=["us"])
                  for bb in range(4):
                      s2 = 0
                      P.dma("sp", [(stt[0:15, s2, :], spool[bb, :, :])], s_st[s2], writes=[("stt", s2)])
                      for gi in range(4):
                          P.op("pe", lambda e: e.transpose(bank(3, 64)[:, gi * 16:gi * 16 + 15], stt[0:15, s2, gi * 128:(gi + 1) * 128], ident[0:15, 0:15]),
                               reads=[("stt", s2), "const"], writes=[("bk", 3)])
                      P.op("dve", lambda e: e.tensor_copy(out=us[:, :, bb, 1:16], in_=bank(3, 64).rearrange("p (g t) -> p g t", g=4)[:, :, 0:15]),
                           reads=[("bk", 3), "us"], writes=["us"])

                  bflat = biasT[:, :, :, :, :].rearrange("p g h k q -> p (g h k q)")
                  chain = {3: ("pool", [t1, t2], ("t0", "t1")), 1: ("pool", [t1, t2], ("t0", "t1")),
                           2: ("dve", [bflat[:, 0:528], bflat[:, 528:1056]], ("t2", "t3")),
                           0: ("dve", [bflat[:, 0:528], bflat[:, 528:1056]], ("t2", "t3"))}

                  def wsum(gi, q0):
                      win = 2 << gi
                      en, bufs, rn = chain[gi]
                      src = uT[:, gi, q0:q0 + 528]
                      sh = 1
                      k = 0
                      while sh < win:
                          dst = bufs[k % 2]
                          lo = 2 * sh
                          P.op(en, lambda e: e.tensor_tensor(out=dst[:, lo:528], in0=src[:, lo:528], in1=src[:, lo - sh:528 - sh], op=ALU.add),
                               reads=[("uT", gi), rn[(k + 1) % 2]] + (["uTpre"] if q0 == 0 else []), writes=[rn[k % 2]])
                          src = dst
                          sh *= 2
                          k += 1
                      P.op("dve", lambda e: e.scalar_tensor_tensor(out=dT[:, gi, q0:q0 + 512], in0=src[:, 16:528], scalar=1.0 / win,
                                                                   in1=uT[:, gi, q0 + 16:q0 + 528], op0=ALU.mult, op1=ALU.subtract),
                           reads=[rn[0], rn[1], ("uT", gi)], writes=["dT"])
                      if q0 == 0:
                          P.op("dve", lambda e: e.tensor_tensor(out=d16[:], in0=src[:, 16:32], in1=invcnt[:, gi * 16:(gi + 1) * 16], op=ALU.mult),
                               reads=[rn[0], rn[1], "const"], writes=["d16"])
                          P.op("dve", lambda e: e.tensor_tensor(out=dT[:, gi, 0:16], in0=d16[:], in1=uT[:, gi, 16:32], op=ALU.subtract),
                               reads=["d16", ("uT", gi), "dT"], writes=["dT"])

                  def wsum_sample(gi):
                      win = 2 << gi
                      en, bufs, rn = chain[gi]
                      ssrc = us[:, gi, :, :]
                      sb2 = [bufs[0][:, 0:96].rearrange("p (b t) -> p b t", b=4), bufs[1][:, 0:96].rearrange("p (b t) -> p b t", b=4)]
                      sh = 1
                      k = 0
                      while sh < win:
                          dst = sb2[k % 2]
                          lo = 2 * sh
                          P.op(en, lambda e: e.tensor_tensor(out=dst[:, :, lo:24], in0=ssrc[:, :, lo:24], in1=ssrc[:, :, lo - sh:24 - sh], op=ALU.add),
                               reads=["us", rn[(k + 1) % 2]], writes=[rn[k % 2]])
                          ssrc = dst
                          sh *= 2
                          k += 1
                      P.op("dve", lambda e: e.scalar_tensor_tensor(out=dT[:, gi, TP:T].rearrange("p (b t) -> p b t", b=4), in0=ssrc[:, :, 16:24], scalar=1.0 / win,
                                                                   in1=us[:, gi, :, 16:24], op0=ALU.mult, op1=ALU.subtract),
                           reads=[rn[0], rn[1], "us"], writes=["dT"])

                  it = 0

                  def u_tile(gi, ti):
                      nonlocal it
                      t0, tn = TT[ti]
                      b = it % 2
                      it += 1
                      for kc in range(KC):
                          P.op("pe", lambda e: e.matmul(bank(b, tn), wu[:, kc, gi * 128:(gi + 1) * 128], hT[:, kc, t0:t0 + tn],
                                                        start=(kc == 0), stop=(kc == KC - 1)), reads=["wu"], writes=[("bk", b)])
                      if ti < 4:
                          P.op("act", lambda e: e.activation(out=uT[:, gi, 16 + t0:16 + t0 + tn], in_=bank(b, tn), func=AF.Copy),
                               reads=[("bk", b)], writes=[("uT", gi)])
                      else:
                          P.op("act", lambda e: e.activation(out=us[:, gi, :, 16:24], in_=bank(b, tn).rearrange("p (b t) -> p b t", t=8), func=AF.Copy),
                               reads=[("bk", b), "us"], writes=["us"])

                  for gi in range(4):
                      u_tile(gi, 3)
                  P.dma("pool", [(bnc_u.ap().rearrange("p (g t) -> p g t", g=4), uT[:, :, 16 + TP - 15:16 + TP])], s_p3,
                        reads=[("uT", gi) for gi in range(4)], writes=["bncu"])
                  s_cu = P.slot("ccu")
                  P.collective(bnc_u.ap().opt(), gat_u.ap().opt(), s_cu, reads=["bncu"], writes=["gatu"])
                  for gi in (3, 2, 1, 0):
                      for ti in (0, 1, 2, 4):
                          u_tile(gi, ti)
                      for q0 in (512, 1024, 1536):
                          wsum(gi, q0)
                      wsum_sample(gi)
                      if gi == 3:
                          P.dma("sp", [(pre[:], gat_u.ap()[0:128, :].rearrange("p (g t) -> p g t", g=4))], s_p4, reads=["gatu"], writes=["pre"])
                          P.op("dve", lambda e: e.tensor_scalar(out=uT[:, :, 1:16], in0=pre[:], scalar1=h01[:, 0:1], scalar2=None, op0=ALU.mult),
                               reads=["pre", "uTpre", "const"], writes=["uTpre"])
                      else:
                          wsum(gi, 0)
                  for gi in range(4):
                      P.op("pe", lambda e: e.transpose(bank(2, 512)[0:15, gi * 128:(gi + 1) * 128], uT[:, gi, 16 + TP - 15:16 + TP], ident[:, :]),
                           reads=[("uT", gi), "const"], writes=[("bk", 2)])
                  P.op("dve", lambda e: e.tensor_copy(out=pout[0:15, :], in_=bank(2, 512)[0:15, :]), reads=[("bk", 2)], writes=["pout"])
                  P.dma("sp", [(poolp[:, :], pout[0:15, :])], s_po, reads=["pout"])
                  for bb in range(4):
                      s2 = 0
                      for gi in range(4):
                          P.op("pe", lambda e: e.transpose(bank(2, 512)[0:8, gi * 128:(gi + 1) * 128], us[:, gi, bb, 16:24], ident[:, :]),
                               reads=["us", "const"], writes=[("bk", 2)])
                      P.op("dve", lambda e: e.tensor_copy(out=pouts[:, s2, :], in_=bank(2, 512)[0:8, :]), reads=[("bk", 2)], writes=[("pouts", s2)])
                      P.dma("sp", [(pools[bb, 7:15, :], pouts[:, s2, :])], s_ps[s2], reads=[("pouts", s2)])
                  wsum(3, 0)
                  for gi in range(4):
                      for ti, (t0, tn) in enumerate(TT):
                          b = it % 2
                          it += 1
                          P.op("pe", lambda e: e.matmul(bank(b, tn), wg[:, gi, :], dT[:, gi, t0:t0 + tn], start=True, stop=True),
                               reads=["wu", "dT"], writes=[("bk", b)])
                          P.op("act", lambda e: e.activation(out=pooledT[:, gi, t0:t0 + tn], in_=bank(b, tn), func=AF.Copy, scale=pscale[:, gi:gi + 1]),
                               reads=[("bk", b), "const2"], writes=["pooledT"])
                  P.barrier()

              with nc.sbuf_tensor("mergedT", [128, KC, T], BF16) as mergedT:
                  with nc.sbuf_tensor("wgt", [128, 2, KC, 2, 256], BF16) as wgt, nc.sbuf_tensor("wab", [128, 2, D], BF16) as wab, \
                          nc.sbuf_tensor("wpb", [128, 4, D], BF16) as wpb, nc.sbuf_tensor("sg", [128, 2, 2, 512], F32) as sg:
                      s_m = P.slot("wab")
                      sgt = [P.slot("wgt0"), P.slot("wgt1")]
                      it = 0
                      def load_wgt(f0):
                          s = (f0 // 2) % 2
                          P.dma("pool", [(wgt[:, s, :, 0, :], w_in[:, 2816 + f0 * 128:2816 + (f0 + 2) * 128].rearrange("(kc p) c -> p kc c", p=128)),
                                         (wgt[:, s, :, 1, :], w_in[:, 3840 + f0 * 128:3840 + (f0 + 2) * 128].rearrange("(kc p) c -> p kc c", p=128))],
                                sgt[s], writes=[("wgt", s)])
                      load_wgt(0)
                      P.dma("pool", [(wab[:], wab_d.rearrange("(c p) f -> p c f", p=128)),
                                     (wpb[:], wpb_d.rearrange("(c p) f -> p c f", p=128))], s_m, writes=["wab"])
                      for f0 in range(0, KC, 2):
                          s = (f0 // 2) % 2
                          if f0 + 2 < KC:
                              load_wgt(f0 + 2)
                          for ff in range(2):
                              fc = f0 + ff
                              for ti, (t0, tn) in enumerate(TT):
                                  b = it % 2
                                  it += 1
                                  bga, bab, bgb, bpb = bank(b, tn), bank(2 + b, tn), bank(4 + b, tn), psF[:, (6) * 512:(6) * 512 + tn]
                                  for kc in range(KC):
                                      P.op("pe", lambda e: e.matmul(bga, wgt[:, s, kc, 0, ff * 128:(ff + 1) * 128], hT[:, kc, t0:t0 + tn],
                                                                    start=(kc == 0), stop=(kc == KC - 1)), reads=[("wgt", s)], writes=[("bk", b)])
                                  for c in range(2):
                                      P.op("pe", lambda e: e.matmul(bab, wab[:, c, fc * 128:(fc + 1) * 128], attnT[:, c, t0:t0 + tn],
                                                                    start=(c == 0), stop=(c == 1)), reads=["wab"], writes=[("bk", 2 + b)])
                                  for kc in range(KC):
                                      P.op("pe", lambda e: e.matmul(bgb, wgt[:, s, kc, 1, ff * 128:(ff + 1) * 128], hT[:, kc, t0:t0 + tn],
                                                                    start=(kc == 0), stop=(kc == KC - 1)), reads=[("wgt", s)], writes=[("bk", 4 + b)])
                                  for c in range(4):
                                      P.op("pe", lambda e: e.matmul(bpb, wpb[:, c, fc * 128:(fc + 1) * 128], pooledT[:, c, t0:t0 + tn],
                                                                    start=(c == 0), stop=(c == 3)), reads=["wab"], writes=["b6"])
                                  P.op("act", lambda e: e.activation(out=sg[:, b, 0, 0:tn], in_=bga, func=AF.Sigmoid), reads=[("bk", b)], writes=[("sga", b)])
                                  P.op("act", lambda e: e.activation(out=sg[:, b, 1, 0:tn], in_=bgb, func=AF.Sigmoid), reads=[("bk", 4 + b)], writes=[("sgb", b)])
                                  P.op("dve", lambda e: e.tensor_tensor(out=sg[:, b, 0, 0:tn], in0=bab, in1=sg[:, b, 0, 0:tn], op=ALU.mult),
                                       reads=[("bk", 2 + b), ("sga", b)], writes=[("sga", b)])
                                  P.op("dve", lambda e: e.tensor_tensor(out=sg[:, b, 1, 0:tn], in0=bpb, in1=sg[:, b, 1, 0:tn], op=ALU.mult),
                                       reads=["b6", ("sgb", b)], writes=[("sgb", b)])
                                  P.op("pool", lambda e: e.tensor_tensor(out=mergedT[:, fc, t0:t0 + tn], in0=sg[:, b, 0, 0:tn], in1=sg[:, b, 1, 0:tn], op=ALU.add),
                                       reads=[("sga", b), ("sgb", b)], writes=[("mg", ti)])
                                  if fc < 6 and ti in (0, 2):
                                      copy_piece(("bk", b))
                      P.barrier()
                  with nc.sbuf_tensor("wo", [128, 2, KC, 256], BF16) as wo:
                      so = [P.slot("wo0"), P.slot("wo1")]
                      it = 0
                      for f0 in range(0, KC, 2):
                          s = (f0 // 2) % 2
                          P.dma("pool", [(wo[:, s, :, :], wout_d[:, f0 * 128:(f0 + 2) * 128].rearrange("(kc p) c -> p kc c", p=128))], so[s], writes=[("wo", s)])
                          for ff in range(2):
                              fc = f0 + ff
                              for ti, (t0, tn) in enumerate(TT):
                                  b = it % 2
                                  it += 1
                                  for kc in range(KC):
                                      P.op("pe", lambda e: e.matmul(bank(b, tn), wo[:, s, kc, ff * 128:(ff + 1) * 128], mergedT[:, kc, t0:t0 + tn],
                                                                    start=(kc == 0), stop=(kc == KC - 1)), reads=[("wo", s)], writes=[("bk", b)])
                                  P.op("dve", lambda e: e.tensor_tensor(out=xT[:, fc, t0:t0 + tn], in0=bank(b, tn), in1=xT[:, fc, t0:t0 + tn], op=ALU.add),
                                       reads=[("bk", b)], writes=[("xT", ti)])
                      P.barrier()

        if STOP >= 5:
            p4()

        ffn(wup2, wdn2, "2", 2)

        while cpieces:
            copy_piece(("bk", 0))
        with nc.sbuf_tensor("yo", [128, 4, D], F32) as yo:
            sy = [P.slot("yo%d" % i) for i in range(4)]
            for j in range(17):
                n = 128 if j < 16 else TS
                s = j % 2
                s4 = j % 4
                for kc in range(KC):
                    P.op("pe", lambda e: e.transpose(psF[0:n, s * 1024 + kc * 128:s * 1024 + (kc + 1) * 128], xT[:, kc, j * 128:j * 128 + n], ident[:, :]),
                         reads=["const"], writes=[("tb", s)])
                if j % 2 == 0:
                    P.op("act", lambda e: e.activation(out=yo[0:n, s4, :], in_=psF[0:n, s * 1024:(s + 1) * 1024], func=AF.Copy),
                         reads=[], writes=[("yo", s4), ("tb", s)])
                else:
                    P.op("dve", lambda e: e.tensor_copy(out=yo[0:n, s4, :], in_=psF[0:n, s * 1024:(s + 1) * 1024]),
                         reads=[], writes=[("yo", s4), ("tb", s)])
                P.dma("sp", [(y_tm[j * 128:j * 128 + n, :], yo[0:n, s4, :])], sy[s4], reads=[("yo", s4)])
        P.barrier()
    return nc


_CACHE = {}


def kernel(x_prompt, x_sample, cache_kv_w128, cache_kv_w512, cache_kv_w2048, state_pool,
           rel_bias_table, norm_ffn1, ffn1_w_up, ffn1_w_down, norm_mix, w_in, q_norm, k_norm,
           pool_w_group, pool_scale, w_attn_branch, w_pool_branch, w_out,
           norm_ffn2, ffn2_w_up, ffn2_w_down):
    f = lambda a: np.ascontiguousarray(np.asarray(a, dtype=np.float32))
    if "nc" not in _CACHE:
        _CACHE["nc"] = build_program()
    nc = _CACHE["nc"]
    oh = _onehots()
    J = np.ascontiguousarray(np.eye(128, dtype=np.float32)[::-1])
    ident = np.eye(128, dtype=np.float32)
    shared = {
        "table": f(rel_bias_table), "oh": oh, "J": J, "ident": ident,
        "g_ffn1": f(norm_ffn1[0]), "g_mix": f(norm_mix[0]), "g_ffn2": f(norm_ffn2[0]),
        "wup1": f(ffn1_w_up[0]), "wdn1": f(ffn1_w_down[0]), "wup2": f(ffn2_w_up[0]), "wdn2": f(ffn2_w_down[0]),
        "w_in": f(w_in[0]), "q_norm": f(q_norm[0]).reshape(768), "k_norm": f(k_norm[0]).reshape(768),
        "wgrp": f(pool_w_group[0]), "pscale": f(pool_scale[0]), "w_ab": f(w_attn_branch[0]),
        "w_pb": f(w_pool_branch[0]), "w_out": f(w_out[0]),
    }
    xp = np.asarray(x_prompt, np.float32)
    xs = np.asarray(x_sample, np.float32)
    in_maps = []
    for c in range(8):
        sq, half = c // 2, c % 2
        m = dict(shared)
        m["x_tm"] = np.ascontiguousarray(np.concatenate(
            [xp[sq, half * TP:(half + 1) * TP], xs[4 * c:4 * c + 4].reshape(TS, D)], axis=0))
        m["c128"] = f(cache_kv_w128[0, 4 * c:4 * c + 4]).reshape(4, 128, 512)
        m["c512"] = f(cache_kv_w512[0, 4 * c:4 * c + 4]).reshape(4, 512, 512)
        m["c2048"] = f(cache_kv_w2048[0, 4 * c:4 * c + 4]).reshape(4, 2048, 512)
        m["spool"] = f(state_pool[0, 4 * c:4 * c + 4])
        m["hmask"] = np.full((128, 1), 0.0 if half else NEG, np.float32)
        m["h01"] = np.full((128, 1), 1.0 if half else 0.0, np.float32)
        ic = np.zeros((128, 64), np.float32)
        for gi in range(4):
            win = 2 << gi
            pos = half * TP + np.arange(16)
            ic[:, gi * 16:(gi + 1) * 16] = 1.0 / np.minimum(pos + 1, win).astype(np.float32)
        m["invcnt"] = ic
        in_maps.append(m)
    res = run_bass_kernel_spmd(nc, in_maps, core_ids=list(range(8)))
    R = res.results
    y_p = np.empty((4, 4096, D), np.float32)
    y_s = np.empty((32, 8, D), np.float32)
    for c in range(8):
        sq, half = c // 2, c % 2
        y_p[sq, half * TP:(half + 1) * TP] = R[c]["y_tm"][0:TP]
        y_s[4 * c:4 * c + 4] = R[c]["y_tm"][TP:T].reshape(4, 8, D)
    kvp = [np.stack([R[2 * s + 1][n] for s in range(4)], 0).reshape(1, 4, w, 2, 4, 64)
           for n, w in (("kvp128", 128), ("kvp512", 512), ("kvp2048", 2048))]
    poolp = np.stack([R[2 * s + 1]["poolp"] for s in range(4)], 0).reshape(1, 4, 15, 512)
    kvs = [np.concatenate([R[c][n] for c in range(8)], 0).reshape(1, 32, w, 2, 4, 64)
           for n, w in (("kvs128", 128), ("kvs512", 512), ("kvs2048", 2048))]
    pools = np.concatenate([R[c]["pools"] for c in range(8)], 0).reshape(1, 32, 15, 512)
    return (y_p, y_s, kvp[0], kvp[1], kvp[2], poolp, kvs[0], kvs[1], kvs[2], pools)
```

```python
import math
from contextlib import ExitStack
import numpy as np
import concourse.bass as bass
import concourse.mybir as mybir
from concourse.bass_utils import run_bass_kernel_spmd

F32 = mybir.dt.float32
BF16 = mybir.dt.bfloat16
ALU = mybir.AluOpType
AF = mybir.ActivationFunctionType
AX = mybir.AxisListType

D = 1024
KC = 8
TP = 2048
TS = 32
T = TP + TS
DFF = 2816
NCH = 22
NEG = -30000.0
EPS = 1e-6
GROUPS = ((128, 1), (512, 4), (2048, 16))
TT = [(0, 512), (512, 512), (1024, 512), (1536, 512), (2048, 32)]
PAIRS = [[0, 1], [2, 3], [4, 5], [6, 7]]
NCLS = (1, 4, 8)
NQ = (8, 2, 1)
WINS = (128, 512, 2048)


def _t5_buckets(distance):
    max_exact = 16
    d = np.asarray(distance, dtype=np.int32)
    ratio = np.log(np.maximum(d, 1).astype(np.float32) / np.float32(max_exact))
    large = max_exact + (ratio / np.float32(math.log(2048 / max_exact)) * (32 - max_exact)).astype(np.int32)
    large = np.minimum(large, 31)
    return np.where(d < max_exact, d, large).astype(np.int32)


def _onehots():
    oh = np.zeros((33, 1200), np.float32)
    for g, (win, dil) in enumerate(GROUPS):
        bk = _t5_buckets(dil * np.arange(129))
        for x in range(384):
            dl = 255 - x
            if 0 <= dl <= 128:
                oh[bk[dl], g * 384 + x] = 1.0
            else:
                oh[32, g * 384 + x] = 1.0
        for x in range(16):
            df = x - 7
            if x < 15 and df >= 0 and df % dil == 0:
                oh[bk[df // dil], 1152 + g * 16 + x] = 1.0
            else:
                oh[32, 1152 + g * 16 + x] = 1.0
    return oh


class _Eng:
    def __init__(self, obj, sem, is_pe=False):
        self.obj = obj
        self.sem = sem
        self.cnt = 0
        self.seen = {}
        self.is_pe = is_pe


class _Slot:
    def __init__(self, sem):
        self.sem = sem
        self.cnt = 0


class _Res:
    __slots__ = ("w", "r")

    def __init__(self):
        self.w = None
        self.r = {}


class Prog:
    def __init__(self, nc, stack):
        self.nc = nc
        self.stack = stack
        self.E = {}
        for name, obj in (("pe", nc.tensor), ("act", nc.scalar), ("dve", nc.vector),
                          ("pool", nc.gpsimd), ("sp", nc.sync)):
            self.E[name] = _Eng(obj, stack.enter_context(nc.semaphore("s_" + name)), name == "pe")
        self.res = {}
        self.slots = []
        self.halt = False

    def slot(self, name):
        s = _Slot(self.stack.enter_context(self.nc.semaphore("d_" + name)))
        self.slots.append(s)
        return s

    def _R(self, k):
        r = self.res.get(k)
        if r is None:
            r = self.res[k] = _Res()
        return r

    def _wait(self, eng, reads, writes):
        deps = {}

        def add(tok, war=False):
            sem, val = tok
            if sem is eng.sem and eng.is_pe:
                return
            k = id(sem)
            if k not in deps or deps[k][1] < val:
                deps[k] = (sem, val)
        for k in reads:
            r = self._R(k)
            if r.w:
                add(r.w)
        for k in writes:
            r = self._R(k)
            if r.w:
                add(r.w)
            for tok in r.r.values():
                add(tok, True)
        for k, (sem, val) in deps.items():
            if eng.seen.get(k, 0) < val:
                eng.obj.wait_ge(sem, val)
                eng.seen[k] = val

    def _mark(self, tok, reads, writes):
        for k in reads:
            self._R(k).r[id(tok[0])] = tok
        for k in writes:
            r = self._R(k)
            r.w = tok
            r.r = {}

    def op(self, en, fn, reads=(), writes=()):
        if self.halt:
            return
        eng = self.E[en]
        self._wait(eng, reads, writes)
        ins = fn(eng.obj)
        eng.cnt += 1
        ins.then_inc(eng.sem, 1)
        self._mark((eng.sem, eng.cnt), reads, writes)

    def dma(self, q, pairs, slot, reads=(), writes=()):
        if self.halt:
            return
        eng = self.E[q]
        self._wait(eng, reads, writes)
        for out, in_ in pairs:
            eng.obj.dma_start(out=out, in_=in_).then_inc(slot.sem, 16)
            slot.cnt += 16
        self._mark((slot.sem, slot.cnt), reads, writes)

    def collective(self, ins_ap, outs_ap, slot, reads=(), writes=()):
        if self.halt:
            return
        eng = self.E["pool"]
        self._wait(eng, reads, writes)
        eng.obj.collective_compute("AllGather", ALU.bypass, replica_groups=PAIRS,
                                   ins=[ins_ap], outs=[outs_ap]).then_inc(slot.sem)
        slot.cnt += 1
        self._mark((slot.sem, slot.cnt), reads, writes)

    def barrier(self):
        toks = [(e.sem, e.cnt) for e in self.E.values() if e.cnt] + [(s.sem, s.cnt) for s in self.slots if s.cnt]
        for e in self.E.values():
            for sem, val in toks:
                if sem is e.sem:
                    continue
                if e.seen.get(id(sem), 0) < val:
                    e.obj.wait_ge(sem, val)
                    e.seen[id(sem)] = val
        self.res = {}


def build_program():
    import os
    STOP = int(os.environ.get("KSTOP", "99"))
    SUB = int(os.environ.get("KSUB", "99"))

    class _Stop(Exception):
        pass

    def chk(n):
        if SUB == n:
            PH[0].halt = True
    PH = [None]
    nc = bass.Bass("TRN2", target_bir_lowering=False)

    def din(name, shape, dt=F32):
        return nc.dram_tensor(name, list(shape), dt, kind="ExternalInput").ap()

    def dout(name, shape):
        return nc.dram_tensor(name, list(shape), F32, kind="ExternalOutput").ap()

    x_tm = din("x_tm", [T, D])
    caches = [din("c128", [4, 128, 512]), din("c512", [4, 512, 512]), din("c2048", [4, 2048, 512])]
    spool = din("spool", [4, 15, 512])
    table = din("table", [32, 12])
    oh_d = din("oh", [33, 1200])
    J_d = din("J", [128, 128])
    id_d = din("ident", [128, 128])
    hmask_d = din("hmask", [128, 1])
    h01_d = din("h01", [128, 1])
    invcnt_d = din("invcnt", [128, 64])
    g_ffn1 = din("g_ffn1", [D])
    g_mix = din("g_mix", [D])
    g_ffn2 = din("g_ffn2", [D])
    wup1 = din("wup1", [D, 2 * DFF])
    wdn1 = din("wdn1", [DFF, D])
    wup2 = din("wup2", [D, 2 * DFF])
    wdn2 = din("wdn2", [DFF, D])
    w_in = din("w_in", [D, 4864])
    qn_d = din("q_norm", [768])
    kn_d = din("k_norm", [768])
    wgrp_d = din("wgrp", [4, 128, 128])
    pscale_d = din("pscale", [512])
    wab_d = din("w_ab", [256, D])
    wpb_d = din("w_pb", [512, D])
    wout_d = din("w_out", [D, D])

    y_tm = dout("y_tm", [T, D])
    kvp = [dout("kvp128", [128, 512]), dout("kvp512", [512, 512]), dout("kvp2048", [2048, 512])]
    poolp = dout("poolp", [15, 512])
    kvs = [dout("kvs128", [4, 128, 512]), dout("kvs512", [4, 512, 512]), dout("kvs2048", [4, 2048, 512])]
    pools = dout("pools", [4, 15, 512])

    ud_t = nc.dram_tensor("ud", [12, 1200], F32)
    ud = ud_t.ap()
    NH = (1, 4, 16)
    bnc, gat = [], []
    for hp in range(2):
        for g in range(3):
            w = NH[g] * 384
            bnc.append(nc.dram_tensor("bnc%d%d" % (hp, g), [128, w], BF16))
            gat.append(nc.dram_tensor("gat%d%d" % (hp, g), [256, w], BF16))
    qsc = [nc.dram_tensor("qsc%d" % hp, [128, TP], BF16) for hp in range(2)]
    bnc_u = nc.dram_tensor("bnc_u", [128, 60], F32)
    gat_u = nc.dram_tensor("gat_u", [256, 60], F32)

    with ExitStack() as st:
        P = Prog(nc, st)
        PH[0] = P

        def sb(name, shape, dt=F32):
            return st.enter_context(nc.sbuf_tensor("t_" + name, list(shape), dt))

        psF = st.enter_context(nc.psum_tensor("psF", [128, 7 * 512], F32))
        psB = st.enter_context(nc.psum_tensor("psB", [128, 1024], BF16))

        def bank(i, n=512):
            return psF[:, i * 512:i * 512 + n]

        xT = sb("xT", [128, KC, T])
        hT = sb("hT", [128, KC, T], BF16)
        attnT = sb("attnT", [128, 2, T], BF16)
        ident = sb("identf", [128, 128])
        identb = sb("identb", [128, 128], BF16)
        onesb = sb("onesb", [128, 128], BF16)
        Jf = sb("Jf", [128, 128])
        gcol = sb("gcol", [128, 3, KC])
        hmask = sb("hmask", [128, 1])
        h01 = sb("h01", [128, 1])
        invcnt = sb("invcnt", [128, 64])
        pscale = sb("pscale", [128, 4])
        biasT = sb("biasT", [128, 3, 4, 2, 128])
        biasS = sb("biasS", [8, 3, 4, 8])

        s_const = P.slot("const")
        P.dma("sp", [(ident[:], id_d[:, :]), (Jf[:], J_d[:, :]), (hmask[:], hmask_d[:, :]),
                     (h01[:], h01_d[:, :]), (invcnt[:], invcnt_d[:, :])], s_const, writes=["const"])
        s_const2 = P.slot("const2")
        s_const3 = P.slot("const3")
        with nc.allow_non_contiguous_dma(reason="tiny one-time gain/scale loads"):
            P.dma("pool", [(gcol[:, 0, :], g_ffn1.rearrange("(c p) -> p c", p=128)),
                         (gcol[:, 1, :], g_mix.rearrange("(c p) -> p c", p=128)),
                         (gcol[:, 2, :], g_ffn2.rearrange("(c p) -> p c", p=128)),
                         (pscale[:], pscale_d.rearrange("(c p) -> p c", p=128))], s_const2, writes=["const2"])
        P.dma("pool", [(identb[:], id_d[:, :])], s_const3, writes=["identb"])
        P.op("dve", lambda e: e.memset(onesb[:], 1.0), writes=["onesb"])

        rr = {"i": 0, "n": 0}

        def copy_op(en, out, in_, reads, writes):
            if en == "act":
                P.op("act", lambda e: e.activation(out=out, in_=in_, func=AF.Copy), reads=reads, writes=writes)
            else:
                P.op(en, lambda e: e.tensor_copy(out=out, in_=in_), reads=reads, writes=writes)

        def alt(*names):
            rr["i"] += 1
            return names[rr["i"] % len(names)]

        def rmsnorm(gi_, bufs=None):
            rr["n"] += 1
            with ExitStack() as ns_:
                if bufs is None:
                    sq = ns_.enter_context(nc.sbuf_tensor("sq%d" % rr["n"], [128, KC, 512], BF16))
                    rstd = ns_.enter_context(nc.sbuf_tensor("rstd%d" % rr["n"], [128, 2, 512], F32))
                else:
                    sq, rstd = bufs
                for ti, (t0, tn) in enumerate(TT):
                    s = ti % 2
                    P.op("act", lambda e: e.activation(out=sq[:, :, 0:tn], in_=xT[:, :, t0:t0 + tn], func=AF.Square),
                         reads=[("xT", ti)], writes=["sq"])
                    for kc in range(KC):
                        P.op("pe", lambda e: e.matmul(bank(6, tn), onesb[:], sq[:, kc, 0:tn], start=(kc == 0), stop=(kc == KC - 1)),
                             reads=["sq", "onesb"], writes=["b6"])
                    P.op("act", lambda e: e.activation(out=rstd[:, s, 0:tn], in_=bank(6, tn), func=AF.Ln, bias=EPS, scale=1.0 / D),
                         reads=["b6"], writes=[("rstd", s)])
                    P.op("act", lambda e: e.activation(out=rstd[:, s, 0:tn], in_=rstd[:, s, 0:tn], func=AF.Exp, scale=-0.5),
                         reads=[("rstd", s)], writes=[("rstd", s)])
                    for kc in range(KC):
                        en = "dve"
                        P.op(en, lambda e: e.scalar_tensor_tensor(out=hT[:, kc, t0:t0 + tn], in0=xT[:, kc, t0:t0 + tn],
                                                                  scalar=gcol[:, gi_, kc:kc + 1], in1=rstd[:, s, 0:tn],
                                                                  op0=ALU.mult, op1=ALU.mult),
                             reads=[("xT", ti), ("rstd", s), "const2"], writes=[("hT", ti)])
                P.barrier()

        cpieces = []
        for g in (2, 1, 0):
            w = WINS[g]
            npc = 4 if g == 2 else 1
            rows = (w - 8) // npc
            for bb in range(4):
                for pc in range(npc):
                    r0 = pc * rows
                    cpieces.append((kvs[g][bb, r0:r0 + rows, :], caches[g][bb, 8 + r0:8 + r0 + rows, :]))
        s_cc = P.slot("ccopy")

        def copy_piece(dep):
            if cpieces:
                d_, s_ = cpieces.pop(0)
                P.dma("sp", [(d_, s_)], s_cc, reads=[dep])

        def ffn(wup, wdn, tag, norm_idx):
            HC = NCH // 2
            with nc.sbuf_tensor("wupb" + tag, [128, 2, KC, 2, 256], BF16) as wupb:
              su = [P.slot("wup0" + tag), P.slot("wup1" + tag)]

              def load_up(hf, c0, s):
                  ncl = min(2, HC - c0) * 128
                  col = (hf * HC + c0) * 128
                  P.dma("pool", [(wupb[:, s, :, 0, 0:ncl], wup[:, col:col + ncl].rearrange("(kc p) c -> p kc c", p=128)),
                                 (wupb[:, s, :, 1, 0:ncl], wup[:, DFF + col:DFF + col + ncl].rearrange("(kc p) c -> p kc c", p=128))],
                        su[s], writes=[("wup", s)])
              load_up(0, 0, 0)
              rmsnorm(norm_idx)
              with nc.sbuf_tensor("actT" + tag, [128, HC, T], BF16) as actT, \
                    nc.sbuf_tensor("wdnb" + tag, [128, 2, HC, 256], BF16) as wdnb, \
                    nc.sbuf_tensor("silu" + tag, [128, 2, 512], F32) as silu:
                sd = [P.slot("wdn0" + tag), P.slot("wdn1" + tag)]
                nb = 0
                nd = 0
                it = 0
                for hf in range(2):
                    for c0 in range(0, HC, 2):
                        ncl = min(2, HC - c0) * 128
                        s = nb % 2
                        nb += 1
                        if not (hf == 0 and c0 == 0):
                            load_up(hf, c0, s)
                        for cc in range(ncl // 128):
                            c = c0 + cc
                            for ti, (t0, tn) in enumerate(TT):
                                b = it % 2
                                it += 1
                                for kc in range(KC):
                                    P.op("pe", lambda e: e.matmul(bank(b, tn), wupb[:, s, kc, 0, cc * 128:(cc + 1) * 128], hT[:, kc, t0:t0 + tn],
                                                                  start=(kc == 0), stop=(kc == KC - 1)),
                                         reads=[("wup", s), ("hT", ti)], writes=[("bk", b)])
                                for kc in range(KC):
                                    P.op("pe", lambda e: e.matmul(bank(2 + b, tn), wupb[:, s, kc, 1, cc * 128:(cc + 1) * 128], hT[:, kc, t0:t0 + tn],
                                                                  start=(kc == 0), stop=(kc == KC - 1)),
                                         reads=[("wup", s), ("hT", ti)], writes=[("bk", 2 + b)])
                                P.op("act", lambda e: e.activation(out=silu[:, b, 0:tn], in_=bank(b, tn), func=AF.Silu),
                                     reads=[("bk", b)], writes=[("silu", b)])
                                P.op("dve", lambda e: e.tensor_tensor(out=actT[:, c, t0:t0 + tn], in0=bank(2 + b, tn), in1=silu[:, b, 0:tn], op=ALU.mult),
                                     reads=[("bk", 2 + b), ("silu", b)], writes=[("actT", c, ti)])
                        if nb % 2 == 0:
                            copy_piece(("bk", b))
                    for f0 in range(0, KC, 2):
                        s = nd % 2
                        nd += 1
                        P.dma("pool", [(wdnb[:, s, :, :], wdn[hf * HC * 128:(hf + 1) * HC * 128, f0 * 128:(f0 + 2) * 128].rearrange("(c p) f -> p c f", p=128))],
                              sd[s], writes=[("wdn", s)])
                        for ff in range(2):
                            fc = f0 + ff
                            for ti, (t0, tn) in enumerate(TT):
                                b = 4 + it % 2
                                it += 1
                                for c in range(HC):
                                    P.op("pe", lambda e: e.matmul(bank(b, tn), wdnb[:, s, c, ff * 128:(ff + 1) * 128], actT[:, c, t0:t0 + tn],
                                                                  start=(c == 0), stop=(c == HC - 1)),
                                         reads=[("wdn", s), ("actT", c, ti)], writes=[("bk", b)])
                                P.op("dve", lambda e: e.scalar_tensor_tensor(out=xT[:, fc, t0:t0 + tn], in0=bank(b, tn), scalar=0.5,
                                                                             in1=xT[:, fc, t0:t0 + tn], op0=ALU.mult, op1=ALU.add),
                                     reads=[("bk", b), ("xT", ti)], writes=[("xT", ti)])
                P.barrier()


        bstack = ExitStack()
        tab = bstack.enter_context(nc.sbuf_tensor("t_tabaug", [33, 12], F32))
        ohs = bstack.enter_context(nc.sbuf_tensor("t_ohs", [33, 1200], F32))
        uds = bstack.enter_context(nc.sbuf_tensor("t_uds", [12, 1200], F32))
        Tp = bstack.enter_context(nc.sbuf_tensor("t_Tp", [128, 12, 256], F32))
        Bp = bstack.enter_context(nc.sbuf_tensor("t_Bp", [8, 12, 8], F32))
        s_b = P.slot("bias")
        s_b2 = P.slot("bias2")
        s_b3 = P.slot("bias3")
        s_tp = P.slot("tp")
        P.op("dve", lambda e: e.memset(tab[:], NEG), writes=["tab"])
        P.dma("pool", [(tab[0:32, :], table[:, :])], s_b, writes=["tab"])
        P.dma("pool", [(ohs[:], oh_d[:, :])], s_b2, writes=["ohs"])
        for (c0, cn) in ((0, 512), (512, 512), (1024, 176)):
            P.op("pe", lambda e: e.matmul(psF[0:12, c0:c0 + cn], tab[:, :], ohs[:, c0:c0 + cn], start=True, stop=True),
                 reads=["tab", "ohs"], writes=[("tb", 0), ("tb", 1)])
        P.op("dve", lambda e: e.tensor_copy(out=uds[:], in_=psF[0:12, 0:1200]), reads=[], writes=["uds", ("tb", 0), ("tb", 1)])
        P.dma("pool", [(ud[:, :], uds[:])], s_b3, reads=["uds"], writes=["ud"])
        prs = []
        for g in range(3):
            for h in range(4):
                k = g * 4 + h
                prs.append((Tp[:, k, :], bass.AP(ud_t, (4 * g + h) * 1200 + g * 384, [[1, 128], [1, 256]])))
                prs.append((Bp[:, k, :], bass.AP(ud_t, (4 * g + h) * 1200 + 1152 + g * 16, [[1, 8], [1, 8]])))
        P.dma("pool", prs, s_tp, reads=["ud"], writes=["Tp"])

        def bias_setup_2():
            for g in range(3):
                for h in range(4):
                    k = g * 4 + h
                    s = k % 2
                    for kb in range(2):
                        P.op("pe", lambda e: e.matmul(bank(4 + s, 256)[:, kb * 128:(kb + 1) * 128], Tp[:, k, kb * 128:(kb + 1) * 128], Jf[:, :],
                                                      start=True, stop=True), reads=["Tp", "const"], writes=[("bk", 4 + s)])
                    P.op("act", lambda e: e.activation(out=biasT[:, g, h, :, :], in_=bank(4 + s, 256).rearrange("p (a b) -> p a b", b=128), func=AF.Copy),
                         reads=[("bk", 4 + s)], writes=["biasT"])
                    P.op("pe", lambda e: e.matmul(bank(6, 8)[0:8, :], Jf[0:8, 120:128], Bp[:, k, :], start=True, stop=True),
                         reads=["Tp", "const"], writes=["b6"])
                    P.op("dve", lambda e: e.tensor_copy(out=biasS[:, g, h, :], in_=bank(6, 8)[0:8, :]), reads=["b6"], writes=["biasS"])
            P.barrier()
            bstack.close()

        with nc.sbuf_tensor("xin", [128, 4, D], F32) as xin:
            sx = [P.slot("xin%d" % i) for i in range(4)]
            for j in range(17):
                n = 128 if j < 16 else TS
                s = j % 2
                s4 = j % 4
                P.dma("sp", [(xin[0:n, s4, :], x_tm[j * 128:j * 128 + n, :])], sx[s4], writes=[("xin", s4)])
                for kc in range(KC):
                    P.op("pe", lambda e: e.transpose(psF[:, s * 1024 + kc * 128:s * 1024 + kc * 128 + n],
                                                     xin[0:n, s4, kc * 128:(kc + 1) * 128], ident[0:n, 0:n]),
                         reads=[("xin", s4), "const"], writes=[("tb", s)])
                copy_op(alt("act", "dve"), xT[:, :, j * 128:j * 128 + n],
                        psF[:, s * 1024:(s + 1) * 1024].rearrange("p (k t) -> p k t", t=128)[:, :, 0:n],
                        [("tb", s)], [("xT", j // 4)])
            P.barrier()

        bias_setup_2()
        ffn(wup1, wdn1, "1", 0)

        def attention():
            with ExitStack() as a:
                def sba(name, shape, dt=F32):
                    return a.enter_context(nc.sbuf_tensor("t_" + name, list(shape), dt))
                NSP = 4
                PQB = [0, 1, 4, 5]
                NSA = 3
                SB = [2, 3, 6]
                gqk = sba("gqk", [128, 2, 128])
                Vaug = sba("Vaug", [128, 16, 2, 128], BF16)
                Vs = sba("Vs", [128, 2, 128], BF16)
                Vh = sba("Vh", [128, 16, 2, 128], BF16)
                kTh = sba("kTh", [128, 16 * 128], BF16)
                kT = sba("kT", [128, TP], BF16)
                qTm = sba("qTm", [128, 2, TP], BF16)
                kTs = sba("kTs", [128, 4, 8], BF16)
                qTms = sba("qTms", [128, 4, 2, 8], BF16)
                acc = sba("acc", [128, 2, TP])
                accs = sba("accs", [128, 2, 32])
                wq = sba("wq", [128, KC, 3, 128], BF16)
                buf = sba("pbuf", [128, NSP, 256])
                ssm = sba("ssm", [128, NSP, 4])
                vst = sba("vst", [128, NSP, 128])
                qkb = sba("qkb", [128, NSP, 256], BF16)
                sE = sba("sE", [128, NSA, 512])
                Pt = sba("Pt", [128, NSA, 512], BF16)
                rec = sE
                sEs = sba("sEs", [128, 2, 32])
                Pts = sba("Pts", [128, 2, 32], BF16)
                sEc = sba("sEc", [32, 2, 8])
                biasS32 = sba("biasS32", [32, 4, 3, 4, 8])
                Pcs = sba("Pcs", [128, 2, 8], BF16)
                ck = sba("ck", [128, 3, 4, 128], BF16)
                Vc = sba("Vc", [128, 3, 4, 2, 128], BF16)
                kTc = sba("kTc", [128, 3, 4 * 128], BF16)

                def late_memsets():
                    P.op("pool", lambda e: e.memset(Vaug[:], 1.0), writes=["Vaug"])
                    P.op("pool", lambda e: e.memset(Vs[:], 1.0), writes=["Vaug"])
                lm = [late_memsets]
                P.op("pool", lambda e: e.memset(biasS32[:], NEG), writes=["biasS32"])
                s_bs = P.slot("biasS32")
                P.dma("sp", [(biasS32[8 * bb:8 * bb + 8, bb, :, :, :].rearrange("p g h q -> p (g h q)"),
                              biasS[:, :, :, :].rearrange("p g h q -> p (g h q)")) for bb in range(4)],
                      s_bs, reads=["biasS"], writes=["biasS32"])
                P.op("pool", lambda e: e.memset(Vc[:], 1.0), writes=[("Vc", 0), ("Vc", 1), ("Vc", 2)])
                P.op("dve", lambda e: e.memset(qTm[:], 0.0), writes=["qTm"])
                P.op("dve", lambda e: e.memset(qTms[:], 0.0), writes=["qTm"])
                P.op("dve", lambda e: e.memset(Pcs[:], 0.0), writes=["Pcs"])

                s_w = P.slot("wq")
                s_g = P.slot("gqk")
                s_k = [P.slot("kst%d" % i) for i in range(NSP)]
                s_v = [P.slot("vst%d" % i) for i in range(NSP)]
                s_x = P.slot("xchg")
                s_cx = P.slot("cc")
                s_h = P.slot("halo")
                s_c = [P.slot("ck0"), P.slot("ck1"), P.slot("ck2")]

                def proj_sa(s, M, lhs_cols, vdst):
                    pq = bank(PQB[s], 384)
                    bk = ("bk", PQB[s])
                    for kc in range(KC):
                        P.op("pe", lambda e: e.matmul(pq[0:M, :], lhs_cols(kc), wq[:, kc, :, :].rearrange("p a b -> p (a b)"),
                                                      start=(kc == 0), stop=(kc == KC - 1)),
                             reads=["wq"], writes=[bk])
                    P.op("act", lambda e: e.activation(out=buf[0:M, s, :], in_=pq[0:M, 0:256], func=AF.Square),
                         reads=[], writes=[("buf", s), bk])
                    P.op("act", lambda e: e.activation(out=vst[0:M, s, :], in_=pq[0:M, 256:384], func=AF.Copy),
                         reads=[], writes=[("vst", s), bk])
                    P.op("dve", lambda e: e.tensor_copy(out=vdst, in_=pq[0:M, 256:384].rearrange("p (h d) -> p h d", h=2)),
                         reads=[], writes=["Vaug", bk])
                    P.op("dve", lambda e: e.tensor_reduce(out=ssm[0:M, s, :], in_=buf[0:M, s, :].rearrange("p (a b) -> p a b", b=64),
                                                          axis=AX.X, op=ALU.add), reads=[("buf", s)], writes=[("ssm", s)])

                def proj_sb(s, M):
                    pq = bank(PQB[s], 384)
                    bk = ("bk", PQB[s])
                    P.op("act", lambda e: e.activation(out=ssm[0:M, s, :], in_=ssm[0:M, s, :], func=AF.Sqrt, bias=EPS, scale=1.0 / 64),
                         reads=[("ssm", s)], writes=[("ssm", s)])
                    P.op("dve", lambda e: e.reciprocal(out=ssm[0:M, s, :], in_=ssm[0:M, s, :]), reads=[("ssm", s)], writes=[("ssm", s)])
                    P.op("dve", lambda e: e.tensor_tensor(out=buf[0:M, s, :].rearrange("p (a b) -> p a b", b=64),
                                                          in0=pq[0:M, 0:256].rearrange("p (a b) -> p a b", b=64),
                                                          in1=ssm[0:M, s, :].unsqueeze(2).to_broadcast([M, 4, 64]), op=ALU.mult),
                         reads=[("ssm", s)], writes=[("buf", s), bk])

                def proj_sc(s, M):
                    P.op("pool", lambda e: e.tensor_tensor(out=buf[0:M, s, :], in0=buf[0:M, s, :], in1=gqk[0:M, :, :].rearrange("p a b -> p (a b)"), op=ALU.mult),
                         reads=[("buf", s), "gqk"], writes=[("buf", s)])
                    P.op("pool", lambda e: e.tensor_copy(out=qkb[0:M, s, :], in_=buf[0:M, s, :]),
                         reads=[("buf", s)], writes=[("qkb", s)])

                def proj_B(s, M, q0dst, q1dst, kdst, dmas, view=None):
                    for w2 in range(2):
                        P.op("pe", lambda e: e.transpose(psB[:, w2 * 128:w2 * 128 + M], qkb[0:M, s, w2 * 128:(w2 + 1) * 128], identb[0:M, 0:M]),
                             reads=[("qkb", s), "identb"], writes=["pB"])
                    vw = view if view is not None else (lambda a: a)
                    P.op("act", lambda e: e.activation(out=q0dst, in_=vw(psB[0:64, 0:M]), func=AF.Copy), reads=[], writes=["qTm", "pB"])
                    P.op("act", lambda e: e.activation(out=q1dst, in_=vw(psB[64:128, 0:M]), func=AF.Copy), reads=[], writes=["qTm", "pB"])
                    P.op("dve", lambda e: e.tensor_copy(out=kdst, in_=vw(psB[:, 128:128 + M])), reads=[], writes=["kT", "pB"])
                    for (dk, dv, p0, pn) in dmas:
                        P.dma("sp", [(dk, buf[p0:p0 + pn, s, 128:256])], s_k[s], reads=[("buf", s)])
                        P.dma("sp", [(dv, vst[p0:p0 + pn, s, :])], s_v[s], reads=[("vst", s)])

                def blk_S(i, g, hp, kTp, kTcur, qc, halo):
                    s = i % NSA
                    pS = bank(SB[s], 512).rearrange("p (h k q) -> p h k q", h=2, k=2)
                    for h in range(2):
                        P.op("pe", lambda e: e.matmul(pS[:, h, 0, :], kTp, qTm[:, h, qc], start=True, stop=True),
                             reads=["kT", "qTm"] + (["halo"] if halo else []), writes=[("bk", SB[s])])
                        P.op("pe", lambda e: e.matmul(pS[:, h, 1, :], kTcur, qTm[:, h, qc], start=True, stop=True),
                             reads=["kT", "qTm"], writes=[("bk", SB[s])])
                    P.op("dve", lambda e: e.tensor_tensor(out=sE[:, s, :], in0=bank(SB[s], 512),
                                                          in1=biasT[:, g, 2 * hp:2 * hp + 2, :, :].rearrange("p h k q -> p (h k q)"), op=ALU.add),
                         reads=[("bk", SB[s]), "biasT"], writes=[("sE", s)])
                    sE4 = sE[:, s, :].rearrange("p (h k q) -> p h k q", h=2, k=2)
                    Pt4 = Pt[:, s, :].rearrange("p (h k q) -> p h k q", h=2, k=2)
                    if halo:
                        P.op("act", lambda e: e.activation(out=Pt4[:, :, 0, :], in_=sE4[:, :, 0, :], func=AF.Exp, bias=hmask[:, 0:1]),
                             reads=[("sE", s), "const"], writes=[("Pt", s)])
                        P.op("act", lambda e: e.activation(out=Pt4[:, :, 1, :], in_=sE4[:, :, 1, :], func=AF.Exp),
                             reads=[("sE", s)], writes=[("Pt", s)])
                    else:
                        P.op("act", lambda e: e.activation(out=Pt[:, s, :], in_=sE[:, s, :], func=AF.Exp),
                             reads=[("sE", s)], writes=[("Pt", s)])

                def blk_V(i, first, Vp, Vcur, acc_view, halo):
                    s = i % NSA
                    b = 4 + i % 2
                    Pt4 = Pt[:, s, :].rearrange("p (h k q) -> p h k q", h=2, k=2)
                    pO = bank(b, 256).rearrange("p (h q) -> p h q", h=2)
                    for h in range(2):
                        P.op("pe", lambda e: e.matmul(pO[:, h, :], Vp(h), Pt4[:, h, 0, :], start=True, stop=False),
                             reads=[("Pt", s), "Vaug"] + (["halo"] if halo else []), writes=[("bk", b)])
                        P.op("pe", lambda e: e.matmul(pO[:, h, :], Vcur(h), Pt4[:, h, 1, :], start=False, stop=True),
                             reads=[("Pt", s), "Vaug"], writes=[("bk", b)])
                    if first:
                        P.op("dve", lambda e: e.tensor_copy(out=acc_view, in_=pO), reads=[("bk", b)], writes=["acc"])
                    else:
                        P.op("dve", lambda e: e.tensor_tensor(out=acc_view, in0=pO, in1=acc_view, op=ALU.add),
                             reads=[("bk", b), "acc"], writes=["acc"])

                def run_blocks(hp, g, dil, nb, blocks, cnt):
                    pend = []
                    for (c, b) in blocks:
                        j = c * nb + b
                        tstart = c + dil * 128 * b
                        tok = slice(tstart, tstart + 127 * dil + 1, dil) if dil > 1 else slice(tstart, tstart + 128)
                        halo = (b == 0)
                        if halo:
                            kTp = kTh[:, c * 128:(c + 1) * 128]
                            Vp = (lambda h, c=c: Vh[:, c, h, :])
                        else:
                            kTp = kT[:, (j - 1) * 128:j * 128]
                            Vp = (lambda h, j=j: Vaug[:, j - 1, h, :])
                        i = cnt[0]
                        cnt[0] += 1
                        blk_S(i, g, hp, kTp, kT[:, j * 128:(j + 1) * 128], slice(j * 128, (j + 1) * 128), halo)
                        pend.append((i, Vp, (lambda h, j=j: Vaug[:, j, h, :]), acc[:, :, tok], halo))
                        if len(pend) > 2:
                            pi_, Vp_, Vc_, av_, hl_ = pend.pop(0)
                            blk_V(pi_, g == 0, Vp_, Vc_, av_, hl_)
                    for (pi_, Vp_, Vc_, av_, hl_) in pend:
                        blk_V(pi_, g == 0, Vp_, Vc_, av_, hl_)

                bcnt = [0]
                s_r = P.slot("reload")
                s_r2 = P.slot("reload2")
                s_q = P.slot("qsc")
                c0f = 2 * 256
                P.dma("pool", [(wq[:, :, w3, :], w_in[:, w3 * 768 + c0f:w3 * 768 + c0f + 128].rearrange("(kc p) c -> p kc c", p=128))
                               for w3 in range(3)], s_w, writes=["wq"])
                P.dma("sp", [(gqk[:, 0, :], bass.AP(qn_d.tensor, c0f, [[0, 128], [1, 128]])),
                             (gqk[:, 1, :], bass.AP(kn_d.tensor, c0f, [[0, 128], [1, 128]]))], s_g, writes=["gqk"])
                P.op("dve", lambda e: e.tensor_scalar(out=gqk[:, 0, :], in0=gqk[:, 0, :], scalar1=0.125, scalar2=None, op0=ALU.mult),
                     reads=["gqk"], writes=["gqk"])
                lm[0]()
                lm[0] = None
                rmsnorm(1, bufs=(Vh[:, :, :, :].rearrange("p j h d -> p (j h d)").rearrange("p (k t) -> p k t", k=KC), sE[:, 0:2, :]))
                first_pass = [True]
                for hp in range(2):
                    for g, part in ((2, "AC"), (0, "AHBCD"), (1, "AHBCD"), (2, "HRD")):
                        win, dil = GROUPS[g]
                        nb = TP // (128 * dil)
                        ncl = dil
                        pi = hp * 3 + g
                        c0 = g * 256 + hp * 128
                        nh = NH[g]
                        bt, gt = bnc[pi], gat[pi]
                        ncs, nq = NCLS[g], NQ[g]
                        nch = (ncs + 3) // 4
                        ncu = min(ncs, 4)
                        units = [(bb, ch) for bb in range(4) for ch in range(nch)]

                        def unit_dma(u):
                            bb, ch = units[u]
                            s = u % 3
                            cl0 = ch * 4
                            ksrc = bass.AP(caches[g].tensor, bb * win * 512 + cl0 * 512 + hp * 128, [[dil * 512, 128], [512, ncu], [1, 128]])
                            vsrc = [bass.AP(caches[g].tensor, bb * win * 512 + cl0 * 512 + 256 + hp * 128 + h * 64, [[dil * 512, 128], [512, ncu], [1, 64]]) for h in range(2)]
                            P.dma("pool", [(ck[:, s, 0:ncu, :], ksrc), (Vc[:, s, 0:ncu, 0, 0:64], vsrc[0]), (Vc[:, s, 0:ncu, 1, 0:64], vsrc[1])],
                                  s_c[s], writes=[("ck", s), ("Vc", s)])

                        if "A" in part:
                            if not first_pass[0]:
                                P.dma("pool", [(wq[:, :, w3, :], w_in[:, w3 * 768 + c0:w3 * 768 + c0 + 128].rearrange("(kc p) c -> p kc c", p=128))
                                               for w3 in range(3)], s_w, writes=["wq"])
                            for u in range(3):
                                unit_dma(u)
                            if not first_pass[0]:
                                P.dma("sp", [(gqk[:, 0, :], bass.AP(qn_d.tensor, c0, [[0, 128], [1, 128]])),
                                             (gqk[:, 1, :], bass.AP(kn_d.tensor, c0, [[0, 128], [1, 128]]))], s_g, writes=["gqk"])
                                P.op("dve", lambda e: e.tensor_scalar(out=gqk[:, 0, :], in0=gqk[:, 0, :], scalar1=0.125, scalar2=None, op0=ALU.mult),
                                     reads=["gqk"], writes=["gqk"])
                            first_pass[0] = False
                            items = []
                            for c in range(ncl):
                                for b in range(nb):
                                    j = c * nb + b
                                    tstart = c + dil * 128 * b
                                    tok = slice(tstart, tstart + 127 * dil + 1, dil) if dil > 1 else slice(tstart, tstart + 128)
                                    dmas = []
                                    if b == nb - 1:
                                        dmas = [(bass.AP(kvp[g].tensor, c * 512 + hp * 128, [[dil * 512, 128], [1, 128]]),
                                                 bass.AP(kvp[g].tensor, c * 512 + 256 + hp * 128, [[dil * 512, 128], [1, 128]]), 0, 128)]
                                    items.append(dict(M=128, lhs=(lambda kc, tok=tok: hT[:, kc, tok]), vdst=Vaug[:, j, :, 0:64],
                                                      q0=qTm[0:64, 0, j * 128:(j + 1) * 128], q1=qTm[64:128, 1, j * 128:(j + 1) * 128],
                                                      k=kT[:, j * 128:(j + 1) * 128], dmas=dmas))
                            items.append(dict(M=32, lhs=(lambda kc: hT[:, kc, TP:T]), vdst=Vs[0:32, :, 0:64],
                                              q0=qTms[0:64, :, 0, :], q1=qTms[64:128, :, 1, :], k=kTs[:, :, :],
                                              view=(lambda a: a.rearrange("p (b t) -> p b t", b=4)),
                                              dmas=[(kvs[g][bb, win - 8:win, hp * 128:(hp + 1) * 128],
                                                     kvs[g][bb, win - 8:win, 256 + hp * 128:256 + (hp + 1) * 128], 8 * bb, 8) for bb in range(4)]))
                            n_it = len(items)
                            for it_ in range(n_it + 3):
                                if it_ < n_it:
                                    proj_sa(it_ % NSP, items[it_]["M"], items[it_]["lhs"], items[it_]["vdst"])
                                if 0 <= it_ - 1 < n_it:
                                    proj_sb((it_ - 1) % NSP, items[it_ - 1]["M"])
                                if 0 <= it_ - 2 < n_it:
                                    proj_sc((it_ - 2) % NSP, items[it_ - 2]["M"])
                                if 0 <= it_ - 3 < n_it:
                                    o = items[it_ - 3]
                                    proj_B((it_ - 3) % NSP, o["M"], o["q0"], o["q1"], o["k"], o["dmas"], o.get("view"))
                            nh = NH[g]
                            bt, gt = bnc[pi], gat[pi]
                            P.dma("sp", [(bt.ap()[:, 0:nh * 128].rearrange("p (c t) -> p c t", c=ncl),
                                          kT[:, :].rearrange("p (c t) -> p c t", c=ncl)[:, :, (nb - 1) * 128:nb * 128]),
                                         (bt.ap()[:, nh * 128:nh * 384].rearrange("p (c x) -> p c x", c=ncl),
                                          Vaug[:, :, :, :].rearrange("p (c b) h d -> p c b (h d)", c=ncl)[:, :, nb - 1, :])],
                                  s_x, reads=["kT", "Vaug"], writes=["bnc"])
                            P.collective(bt.ap().opt(), gt.ap().opt(), s_cx, reads=["bnc"], writes=["gat"])
                            if g == 2:
                                P.dma("sp", [(qsc[hp].ap()[0:64, :], qTm[0:64, 0, :]), (qsc[hp].ap()[64:128, :], qTm[64:128, 1, :])],
                                      s_q, reads=["qTm"], writes=["qsc"])
                        if "H" in part:
                            P.dma("sp", [(kTh[:, 0:nh * 128], gt.ap()[0:128, 0:nh * 128]),
                                           (Vh[:, 0:nh, :, :].rearrange("p j h d -> p (j h d)"), gt.ap()[0:128, nh * 128:nh * 384])],
                                  s_h, reads=["gat"], writes=["halo"])
                        if "R" in part:
                            P.dma("sp", [(kT[:, :], bt.ap()[:, 0:TP]),
                                         (qTm[0:64, 0, :], qsc[hp].ap()[0:64, :]), (qTm[64:128, 1, :], qsc[hp].ap()[64:128, :])],
                                  s_r, reads=["bnc", "qsc"], writes=["kT", "qTm"])
                            P.dma("sp", [(Vaug[:, :, :, :].rearrange("p j h d -> p (j h d)"), bt.ap()[:, TP:3 * TP])],
                                  s_r2, reads=["bnc"], writes=["Vaug"])
                        if "B" in part:
                            run_blocks(hp, g, dil, nb, [(c, b) for c in range(ncl) for b in range(1, nb)], bcnt)
                        if "C" in part:

                            def unit_T(u, bb, ch):
                                s = u % 3
                                for c in range(ncu):
                                    P.op("pe", lambda e: e.transpose(psB[:, c * 128:(c + 1) * 128], ck[:, s, c, :], identb[:, :]),
                                         reads=[("ck", s), "identb"], writes=["pB"])
                                P.op("dve", lambda e: e.tensor_copy(out=kTc[:, s, 0:ncu * 128], in_=psB[:, 0:ncu * 128]),
                                     reads=[], writes=[("kTc", s), "pB"])

                            def unit_S(u, bb, ch):
                                s = u % 2
                                s3 = u % 3
                                cl0 = ch * 4
                                pSp = bank(0, 2 * ncu * nq).rearrange("p (h c q) -> p h c q", h=2, c=ncu)
                                pSc = bank(0, 512)[0:32, 64:80].rearrange("p (h q) -> p h q", h=2)
                                for h in range(2):
                                    if ch == 0:
                                        P.op("pe", lambda e: e.matmul(pSc[:, h, :], kTs[:, :, :].rearrange("p b t -> p (b t)"), qTms[:, bb, h, :], start=True, stop=True),
                                             reads=["kT", "qTm"], writes=[("bk", 0)])
                                    for c in range(ncu):
                                        cl = cl0 + c
                                        P.op("pe", lambda e: e.matmul(pSp[:, h, c, :], kTc[:, s3, c * 128:(c + 1) * 128], qTms[:, bb, h, cl:8:dil],
                                                                      start=True, stop=True), reads=["qTm", ("kTc", s3)], writes=[("bk", 0)])
                                sEp = sEs[:, s, 0:2 * ncu * nq].rearrange("p (h c q) -> p h c q", h=2, c=ncu)
                                Pp = Pts[:, s, 0:2 * ncu * nq].rearrange("p (h c q) -> p h c q", h=2, c=ncu)
                                bsrc = biasT[:, g, 2 * hp:2 * hp + 2, 0, 0:nq].unsqueeze(2).to_broadcast([128, 2, ncu, nq])
                                P.op("dve", lambda e: e.tensor_tensor(out=sEp, in0=pSp, in1=bsrc, op=ALU.add),
                                     reads=[("bk", 0), "biasT"], writes=[("sEs", s)])
                                P.op("act", lambda e: e.activation(out=Pp, in_=sEp, func=AF.Exp), reads=[("sEs", s)], writes=[("Pts", s)])
                                if ch == 0:
                                    P.op("dve", lambda e: e.tensor_tensor(out=sEc[:], in0=pSc, in1=biasS32[:, bb, g, 2 * hp:2 * hp + 2, :], op=ALU.add),
                                         reads=[("bk", 0), "biasS32"], writes=["sEc"])
                                    P.op("act", lambda e: e.activation(out=Pcs[0:32, :, :], in_=sEc[:], func=AF.Exp), reads=["sEc"], writes=["Pcs"])

                            def unit_V(u, bb, ch):
                                s = u % 2
                                s3 = u % 3
                                cl0 = ch * 4
                                Pp = Pts[:, s, 0:2 * ncu * nq].rearrange("p (h c q) -> p h c q", h=2, c=ncu)
                                pOs = psF[:, 2048:3072].rearrange("p (h x) -> p h x", h=2)[:, :, 0:8]
                                for h in range(2):
                                    if ch == 0:
                                        P.op("pe", lambda e: e.matmul(pOs[:, h, :], Vs[:, h, :], Pcs[:, h, :], start=True, stop=False),
                                             reads=["Pcs", "Vaug"], writes=[("bk", 4), ("bk", 5)])
                                    for c in range(ncu):
                                        cl = cl0 + c
                                        P.op("pe", lambda e: e.matmul(pOs[:, h, cl:8:dil], Vc[:, s3, c, h, :], Pp[:, h, c, :], start=False,
                                                                      stop=(ch == nch - 1 and c == ncu - 1)),
                                             reads=[("Pts", s), ("Vc", s3)], writes=[("bk", 4), ("bk", 5)])
                                if ch == nch - 1:
                                    av = accs[:, :, bb * 8:(bb + 1) * 8]
                                    if g == 2:
                                        P.op("dve", lambda e: e.tensor_copy(out=av, in_=pOs), reads=[("bk", 4), ("bk", 5)], writes=["accs"])
                                    else:
                                        P.op("dve", lambda e: e.tensor_tensor(out=av, in0=pOs, in1=av, op=ALU.add), reads=[("bk", 4), ("bk", 5), "accs"], writes=["accs"])

                            for u in range(min(2, len(units))):
                                unit_T(u, *units[u])
                            for u, (bb, ch) in enumerate(units):
                                unit_S(u, bb, ch)
                                if u + 2 < len(units):
                                    unit_T(u + 2, *units[u + 2])
                                unit_V(u, bb, ch)
                                if u + 3 < len(units):
                                    unit_dma(u + 3)
                        if "D" in part:
                            run_blocks(hp, g, dil, nb, [(c, 0) for c in range(ncl)], bcnt)
                    for q0 in range(0, TP, 512):
                        for h in range(2):
                            P.op("act", lambda e: e.activation(out=rec[0:64, h, :], in_=acc[64:128, h, q0:q0 + 512], func=AF.Ln),
                                 reads=["acc"], writes=[("sE", h)])
                            P.op("act", lambda e: e.activation(out=rec[0:64, h, :], in_=rec[0:64, h, :], func=AF.Exp, scale=-1.0),
                                 reads=[("sE", h)], writes=[("sE", h)])
                        for h in range(2):
                            P.op("dve", lambda e: e.tensor_tensor(out=attnT[h * 64:(h + 1) * 64, hp, q0:q0 + 512], in0=acc[0:64, h, q0:q0 + 512],
                                                                  in1=rec[0:64, h, :], op=ALU.mult), reads=["acc", ("sE", h)], writes=["attnT"])
                    P.op("dve", lambda e: e.reciprocal(out=rec[0:64, 2, 0:64].rearrange("p (h q) -> p h q", h=2), in_=accs[64:128, :, :]),
                         reads=["accs"], writes=[("sE", 2)])
                    for h in range(2):
                        P.op("dve", lambda e: e.tensor_tensor(out=attnT[h * 64:(h + 1) * 64, hp, TP:T], in0=accs[0:64, h, :],
                                                              in1=rec[0:64, 2, h * 32:(h + 1) * 32], op=ALU.mult), reads=["accs", ("sE", 2)], writes=["attnT"])
                P.barrier()

        if STOP >= 4:
            attention()
            P.halt = False
            P.barrier()

        def p4():
          with nc.sbuf_tensor("pooledT", [128, 4, T], BF16) as pooledT:
              with ExitStack() as a:
                  def sba(name, shape, dt=F32):
                      return a.enter_context(nc.sbuf_tensor("t_" + name, list(shape), dt))
                  uT = sba("uT", [128, 4, 16 + TP])
                  us = sba("us", [128, 4, 4, 24])
                  dT = sba("dT", [128, 4, T], BF16)
                  wu = sba("wu", [128, KC, 512], BF16)
                  wg = sba("wg", [128, 4, 128], BF16)
                  t1 = sba("t1", [128, 528])
                  t2 = sba("t2", [128, 528])
                  pre = sba("pre", [128, 4, 15])
                  stt = sba("stt", [16, 1, 512])
                  pout = sba("pout", [16, 512])
                  pouts = sba("pouts", [8, 1, 512])
                  d16 = sba("d16", [128, 16])
                  s_p = P.slot("pool")
                  s_st = [P.slot("stt0"), P.slot("stt1")]
                  s_p3 = P.slot("pool3")
                  s_p4 = P.slot("pool4")
                  s_po = P.slot("poolout")
                  s_ps = [P.slot("pouts0"), P.slot("pouts1")]
                  P.dma("pool", [(wu[:], w_in[:, 2304:2816].rearrange("(kc p) c -> p kc c", p=128)),
                                 (wg[:], wgrp_d.rearrange("g c d -> c g d"))], s_p, writes=["wu"])
                  P.dma("sp", [(pools[bb, 0:7, :], spool[bb, 8:15, :]) for bb in range(4)], s_po)
                  P.op("pool", lambda e: e.memset(uT[:, :, 0:16], 0.0), writes=["uTpre"])
                  P.op("pool", lambda e: e.memset(us[:], 0.0), writes=["us"])
                  for bb in range(4):
                      s2 = 0
                      P.dma("sp", [(stt[0:15, s2, :], spool[bb, :, :])], s_st[s2], writes=[("stt", s2)])
                      for gi in range(4):
                          P.op("pe", lambda e: e.transpose(bank(3, 64)[:, gi * 16:gi * 16 + 15], stt[0:15, s2, gi * 128:(gi + 1) * 128], ident[0:15, 0:15]),
                               reads=[("stt", s2), "const"], writes=[("bk", 3)])
                      P.op("dve", lambda e: e.tensor_copy(out=us[:, :, bb, 1:16], in_=bank(3, 64).rearrange("p (g t) -> p g t", g=4)[:, :, 0:15]),
                           reads=[("bk", 3), "us"], writes=["us"])

                  bflat = biasT[:, :, :, :, :].rearrange("p g h k q -> p (g h k q)")
                  chain = {3: ("pool", [t1, t2], ("t0", "t1")), 1: ("pool", [t1, t2], ("t0", "t1")),
                           2: ("dve", [bflat[:, 0:528], bflat[:, 528:1056]], ("t2", "t3")),
                           0: ("dve", [bflat[:, 0:528], bflat[:, 528:1056]], ("t2", "t3"))}

                  def wsum(gi, q0):
                      win = 2 << gi
                      en, bufs, rn = chain[gi]
                      src = uT[:, gi, q0:q0 + 528]
                      sh = 1
                      k = 0
                      while sh < win:
                          dst = bufs[k % 2]
                          lo = 2 * sh
                          P.op(en, lambda e: e.tensor_tensor(out=dst[:, lo:528], in0=src[:, lo:528], in1=src[:, lo - sh:528 - sh], op=ALU.add),
                               reads=[("uT", gi), rn[(k + 1) % 2]] + (["uTpre"] if q0 == 0 else []), writes=[rn[k % 2]])
                          src = dst
                          sh *= 2
                          k += 1
                      P.op("dve", lambda e: e.scalar_tensor_tensor(out=dT[:, gi, q0:q0 + 512], in0=src[:, 16:528], scalar=1.0 / win,
                                                                   in1=uT[:, gi, q0 + 16:q0 + 528], op0=ALU.mult, op1=ALU.subtract),
                           reads=[rn[0], rn[1], ("uT", gi)], writes=["dT"])
                      if q0 == 0:
                          P.op("dve", lambda e: e.tensor_tensor(out=d16[:], in0=src[:, 16:32], in1=invcnt[:, gi * 16:(gi + 1) * 16], op=ALU.mult),
                               reads=[rn[0], rn[1], "const"], writes=["d16"])
                          P.op("dve", lambda e: e.tensor_tensor(out=dT[:, gi, 0:16], in0=d16[:], in1=uT[:, gi, 16:32], op=ALU.subtract),
                               reads=["d16", ("uT", gi), "dT"], writes=["dT"])

                  def wsum_sample(gi):
                      win = 2 << gi
                      en, bufs, rn = chain[gi]
                      ssrc = us[:, gi, :, :]
                      sb2 = [bufs[0][:, 0:96].rearrange("p (b t) -> p b t", b=4), bufs[1][:, 0:96].rearrange("p (b t) -> p b t", b=4)]
                      sh = 1
                      k = 0
                      while sh < win:
                          dst = sb2[k % 2]
                          lo = 2 * sh
                          P.op(en, lambda e: e.tensor_tensor(out=dst[:, :, lo:24], in0=ssrc[:, :, lo:24], in1=ssrc[:, :, lo - sh:24 - sh], op=ALU.add),
                               reads=["us", rn[(k + 1) % 2]], writes=[rn[k % 2]])
                          ssrc = dst
                          sh *= 2
                          k += 1
                      P.op("dve", lambda e: e.scalar_tensor_tensor(out=dT[:, gi, TP:T].rearrange("p (b t) -> p b t", b=4), in0=ssrc[:, :, 16:24], scalar=1.0 / win,
                                                                   in1=us[:, gi, :, 16:24], op0=ALU.mult, op1=ALU.subtract),
                           reads=[rn[0], rn[1], "us"], writes=["dT"])

                  it = 0

                  def u_tile(gi, ti):
                      nonlocal it
                      t0, tn = TT[ti]
                      b = it % 2
                      it += 1
                      for kc in range(KC):
                          P.op("pe", lambda e: e.matmul(bank(b, tn), wu[:, kc, gi * 128:(gi + 1) * 128], hT[:, kc, t0:t0 + tn],
                                                        start=(kc == 0), stop=(kc == KC - 1)), reads=["wu"], writes=[("bk", b)])
                      if ti < 4:
                          P.op("act", lambda e: e.activation(out=uT[:, gi, 16 + t0:16 + t0 + tn], in_=bank(b, tn), func=AF.Copy),
                               reads=[("bk", b)], writes=[("uT", gi)])
                      else:
                          P.op("act", lambda e: e.activation(out=us[:, gi, :, 16:24], in_=bank(b, tn).rearrange("p (b t) -> p b t", t=8), func=AF.Copy),
                               reads=[("bk", b), "us"], writes=["us"])

                  for gi in range(4):
                      u_tile(gi, 3)
                  P.dma("pool", [(bnc_u.ap().rearrange("p (g t) -> p g t", g=4), uT[:, :, 16 + TP - 15:16 + TP])], s_p3,
                        reads=[("uT", gi) for gi in range(4)], writes=["bncu"])
                  s_cu = P.slot("ccu")
                  P.collective(bnc_u.ap().opt(), gat_u.ap().opt(), s_cu, reads=["bncu"], writes=["gatu"])
                  for gi in (3, 2, 1, 0):
                      for ti in (0, 1, 2, 4):
                          u_tile(gi, ti)
                      for q0 in (512, 1024, 1536):
                          wsum(gi, q0)
                      wsum_sample(gi)
                      if gi == 3:
                          P.dma("sp", [(pre[:], gat_u.ap()[0:128, :].rearrange("p (g t) -> p g t", g=4))], s_p4, reads=["gatu"], writes=["pre"])
                          P.op("dve", lambda e: e.tensor_scalar(out=uT[:, :, 1:16], in0=pre[:], scalar1=h01[:, 0:1], scalar2=None, op0=ALU.mult),
                               reads=["pre", "uTpre", "const"], writes=["uTpre"])
                      else:
                          wsum(gi, 0)
                  for gi in range(4):
                      P.op("pe", lambda e: e.transpose(bank(2, 512)[0:15, gi * 128:(gi + 1) * 128], uT[:, gi, 16 + TP - 15:16 + TP], ident[:, :]),
                           reads=[("uT", gi), "const"], writes=[("bk", 2)])
                  P.op("dve", lambda e: e.tensor_copy(out=pout[0:15, :], in_=bank(2, 512)[0:15, :]), reads=[("bk", 2)], writes=["pout"])
                  P.dma("sp", [(poolp[:, :], pout[0:15, :])], s_po, reads=["pout"])
                  for bb in range(4):
                      s2 = 0
                      for gi in range(4):
                          P.op("pe", lambda e: e.transpose(bank(2, 512)[0:8, gi * 128:(gi + 1) * 128], us[:, gi, bb, 16:24], ident[:, :]),
                               reads=["us", "const"], writes=[("bk", 2)])
                      P.op("dve", lambda e: e.tensor_copy(out=pouts[:, s2, :], in_=bank(2, 512)[0:8, :]), reads=[("bk", 2)], writes=[("pouts", s2)])
                      P.dma("sp", [(pools[bb, 7:15, :], pouts[:, s2, :])], s_ps[s2], reads=[("pouts", s2)])
                  wsum(3, 0)
                  for gi in range(4):
                      for ti, (t0, tn) in enumerate(TT):
                          b = it % 2
                          it += 1
                          P.op("pe", lambda e: e.matmul(bank(b, tn), wg[:, gi, :], dT[:, gi, t0:t0 + tn], start=True, stop=True),
                               reads=["wu", "dT"], writes=[("bk", b)])
                          P.op("act", lambda e: e.activation(out=pooledT[:, gi, t0:t0 + tn], in_=bank(b, tn), func=AF.Copy, scale=pscale[:, gi:gi + 1]),
                               reads=[("bk", b), "const2"], writes=["pooledT"])
                  P.barrier()

              with nc.sbuf_tensor("mergedT", [128, KC, T], BF16) as mergedT:
                  with nc.sbuf_tensor("wgt", [128, 2, KC, 2, 256], BF16) as wgt, nc.sbuf_tensor("wab", [128, 2, D], BF16) as wab, \
                          nc.sbuf_tensor("wpb", [128, 4, D], BF16) as wpb, nc.sbuf_tensor("sg", [128, 2, 2, 512], F32) as sg:
                      s_m = P.slot("wab")
                      sgt = [P.slot("wgt0"), P.slot("wgt1")]
                      it = 0
                      def load_wgt(f0):
                          s = (f0 // 2) % 2
                          P.dma("pool", [(wgt[:, s, :, 0, :], w_in[:, 2816 + f0 * 128:2816 + (f0 + 2) * 128].rearrange("(kc p) c -> p kc c", p=128)),
                                         (wgt[:, s, :, 1, :], w_in[:, 3840 + f0 * 128:3840 + (f0 + 2) * 128].rearrange("(kc p) c -> p kc c", p=128))],
                                sgt[s], writes=[("wgt", s)])
                      load_wgt(0)
                      P.dma("pool", [(wab[:], wab_d.rearrange("(c p) f -> p c f", p=128)),
                                     (wpb[:], wpb_d.rearrange("(c p) f -> p c f", p=128))], s_m, writes=["wab"])
                      for f0 in range(0, KC, 2):
                          s = (f0 // 2) % 2
                          if f0 + 2 < KC:
                              load_wgt(f0 + 2)
                          for ff in range(2):
                              fc = f0 + ff
                              for ti, (t0, tn) in enumerate(TT):
                                  b = it % 2
                                  it += 1
                                  bga, bab, bgb, bpb = bank(b, tn), bank(2 + b, tn), bank(4 + b, tn), psF[:, (6) * 512:(6) * 512 + tn]
                                  for kc in range(KC):
                                      P.op("pe", lambda e: e.matmul(bga, wgt[:, s, kc, 0, ff * 128:(ff + 1) * 128], hT[:, kc, t0:t0 + tn],
                                                                    start=(kc == 0), stop=(kc == KC - 1)), reads=[("wgt", s)], writes=[("bk", b)])
                                  for c in range(2):
                                      P.op("pe", lambda e: e.matmul(bab, wab[:, c, fc * 128:(fc + 1) * 128], attnT[:, c, t0:t0 + tn],
                                                                    start=(c == 0), stop=(c == 1)), reads=["wab"], writes=[("bk", 2 + b)])
                                  for kc in range(KC):
                                      P.op("pe", lambda e: e.matmul(bgb, wgt[:, s, kc, 1, ff * 128:(ff + 1) * 128], hT[:, kc, t0:t0 + tn],
                                                                    start=(kc == 0), stop=(kc == KC - 1)), reads=[("wgt", s)], writes=[("bk", 4 + b)])
                                  for c in range(4):
                                      P.op("pe", lambda e: e.matmul(bpb, wpb[:, c, fc * 128:(fc + 1) * 128], pooledT[:, c, t0:t0 + tn],
                                                                    start=(c == 0), stop=(c == 3)), reads=["wab"], writes=["b6"])
                                  P.op("act", lambda e: e.activation(out=sg[:, b, 0, 0:tn], in_=bga, func=AF.Sigmoid), reads=[("bk", b)], writes=[("sga", b)])
                                  P.op("act", lambda e: e.activation(out=sg[:, b, 1, 0:tn], in_=bgb, func=AF.Sigmoid), reads=[("bk", 4 + b)], writes=[("sgb", b)])
                                  P.op("dve", lambda e: e.tensor_tensor(out=sg[:, b, 0, 0:tn], in0=bab, in1=sg[:, b, 0, 0:tn], op=ALU.mult),
                                       reads=[("bk", 2 + b), ("sga", b)], writes=[("sga", b)])
                                  P.op("dve", lambda e: e.tensor_tensor(out=sg[:, b, 1, 0:tn], in0=bpb, in1=sg[:, b, 1, 0:tn], op=ALU.mult),
                                       reads=["b6", ("sgb", b)], writes=[("sgb", b)])
                                  P.op("pool", lambda e: e.tensor_tensor(out=mergedT[:, fc, t0:t0 + tn], in0=sg[:, b, 0, 0:tn], in1=sg[:, b, 1, 0:tn], op=ALU.add),
                                       reads=[("sga", b), ("sgb", b)], writes=[("mg", ti)])
                                  if fc < 6 and ti in (0, 2):
                                      copy_piece(("bk", b))
                      P.barrier()
                  with nc.sbuf_tensor("wo", [128, 2, KC, 256], BF16) as wo:
                      so = [P.slot("wo0"), P.slot("wo1")]
                      it = 0
                      for f0 in range(0, KC, 2):
                          s = (f0 // 2) % 2
                          P.dma("pool", [(wo[:, s, :, :], wout_d[:, f0 * 128:(f0 + 2) * 128].rearrange("(kc p) c -> p kc c", p=128))], so[s], writes=[("wo", s)])
                          for ff in range(2):
                              fc = f0 + ff
                              for ti, (t0, tn) in enumerate(TT):
                                  b = it % 2
                                  it += 1
                                  for kc in range(KC):
                                      P.op("pe", lambda e: e.matmul(bank(b, tn), wo[:, s, kc, ff * 128:(ff + 1) * 128], mergedT[:, kc, t0:t0 + tn],
                                                                    start=(kc == 0), stop=(kc == KC - 1)), reads=[("wo", s)], writes=[("bk", b)])
                                  P.op("dve", lambda e: e.tensor_tensor(out=xT[:, fc, t0:t0 + tn], in0=bank(b, tn), in1=xT[:, fc, t0:t0 + tn], op=ALU.add),
                                       reads=[("bk", b)], writes=[("xT", ti)])
                      P.barrier()

        if STOP >= 5:
            p4()

        ffn(wup2, wdn2, "2", 2)

        while cpieces:
            copy_piece(("bk", 0))
        with nc.sbuf_tensor("yo", [128, 4, D], F32) as yo:
            sy = [P.slot("yo%d" % i) for i in range(4)]
            for j in range(17):
                n = 128 if j < 16 else TS
                s = j % 2
                s4 = j % 4
                for kc in range(KC):
                    P.op("pe", lambda e: e.transpose(psF[0:n, s * 1024 + kc * 128:s * 1024 + (kc + 1) * 128], xT[:, kc, j * 128:j * 128 + n], ident[:, :]),
                         reads=["const"], writes=[("tb", s)])
                if j % 2 == 0:
                    P.op("act", lambda e: e.activation(out=yo[0:n, s4, :], in_=psF[0:n, s * 1024:(s + 1) * 1024], func=AF.Copy),
                         reads=[], writes=[("yo", s4), ("tb", s)])
                else:
                    P.op("dve", lambda e: e.tensor_copy(out=yo[0:n, s4, :], in_=psF[0:n, s * 1024:(s + 1) * 1024]),
                         reads=[], writes=[("yo", s4), ("tb", s)])
                P.dma("sp", [(y_tm[j * 128:j * 128 + n, :], yo[0:n, s4, :])], sy[s4], reads=[("yo", s4)])
        P.barrier()
    return nc


_CACHE = {}


def kernel(x_prompt, x_sample, cache_kv_w128, cache_kv_w512, cache_kv_w2048, state_pool,
           rel_bias_table, norm_ffn1, ffn1_w_up, ffn1_w_down, norm_mix, w_in, q_norm, k_norm,
           pool_w_group, pool_scale, w_attn_branch, w_pool_branch, w_out,
           norm_ffn2, ffn2_w_up, ffn2_w_down):
    f = lambda a: np.ascontiguousarray(np.asarray(a, dtype=np.float32))
    if "nc" not in _CACHE:
        _CACHE["nc"] = build_program()
    nc = _CACHE["nc"]
    oh = _onehots()
    J = np.ascontiguousarray(np.eye(128, dtype=np.float32)[::-1])
    ident = np.eye(128, dtype=np.float32)
    shared = {
        "table": f(rel_bias_table), "oh": oh, "J": J, "ident": ident,
        "g_ffn1": f(norm_ffn1[0]), "g_mix": f(norm_mix[0]), "g_ffn2": f(norm_ffn2[0]),
        "wup1": f(ffn1_w_up[0]), "wdn1": f(ffn1_w_down[0]), "wup2": f(ffn2_w_up[0]), "wdn2": f(ffn2_w_down[0]),
        "w_in": f(w_in[0]), "q_norm": f(q_norm[0]).reshape(768), "k_norm": f(k_norm[0]).reshape(768),
        "wgrp": f(pool_w_group[0]), "pscale": f(pool_scale[0]), "w_ab": f(w_attn_branch[0]),
        "w_pb": f(w_pool_branch[0]), "w_out": f(w_out[0]),
    }
    xp = np.asarray(x_prompt, np.float32)
    xs = np.asarray(x_sample, np.float32)
    in_maps = []
    for c in range(8):
        sq, half = c // 2, c % 2
        m = dict(shared)
        m["x_tm"] = np.ascontiguousarray(np.concatenate(
            [xp[sq, half * TP:(half + 1) * TP], xs[4 * c:4 * c + 4].reshape(TS, D)], axis=0))
        m["c128"] = f(cache_kv_w128[0, 4 * c:4 * c + 4]).reshape(4, 128, 512)
        m["c512"] = f(cache_kv_w512[0, 4 * c:4 * c + 4]).reshape(4, 512, 512)
        m["c2048"] = f(cache_kv_w2048[0, 4 * c:4 * c + 4]).reshape(4, 2048, 512)
        m["spool"] = f(state_pool[0, 4 * c:4 * c + 4])
        m["hmask"] = np.full((128, 1), 0.0 if half else NEG, np.float32)
        m["h01"] = np.full((128, 1), 1.0 if half else 0.0, np.float32)
        ic = np.zeros((128, 64), np.float32)
        for gi in range(4):
            win = 2 << gi
            pos = half * TP + np.arange(16)
            ic[:, gi * 16:(gi + 1) * 16] = 1.0 / np.minimum(pos + 1, win).astype(np.float32)
        m["invcnt"] = ic
        in_maps.append(m)
    res = run_bass_kernel_spmd(nc, in_maps, core_ids=list(range(8)))
    R = res.results
    y_p = np.empty((4, 4096, D), np.float32)
    y_s = np.empty((32, 8, D), np.float32)
    for c in range(8):
        sq, half = c // 2, c % 2
        y_p[sq, half * TP:(half + 1) * TP] = R[c]["y_tm"][0:TP]
        y_s[4 * c:4 * c + 4] = R[c]["y_tm"][TP:T].reshape(4, 8, D)
    kvp = [np.stack([R[2 * s + 1][n] for s in range(4)], 0).reshape(1, 4, w, 2, 4, 64)
           for n, w in (("kvp128", 128), ("kvp512", 512), ("kvp2048", 2048))]
    poolp = np.stack([R[2 * s + 1]["poolp"] for s in range(4)], 0).reshape(1, 4, 15, 512)
    kvs = [np.concatenate([R[c][n] for c in range(8)], 0).reshape(1, 32, w, 2, 4, 64)
           for n, w in (("kvs128", 128), ("kvs512", 512), ("kvs2048", 2048))]
    pools = np.concatenate([R[c]["pools"] for c in range(8)], 0).reshape(1, 32, 15, 512)
    return (y_p, y_s, kvp[0], kvp[1], kvp[2], poolp, kvs[0], kvs[1], kvs[2], pools)
```
